# Optimizing a Trainium2 kernel written in Bass

```python
import math
import jax, jax.numpy as jnp
from jax import lax
import numpy as np

D_MODEL = 1024
BATCH = 1
SEQ = 16384
DEPTH = 1

D_MIX = D_MODEL
ATTN_WIDTH = D_MIX // 2
HEAD_DIM = 64
N_HEADS = ATTN_WIDTH // HEAD_DIM
WINDOWS = (128, 512, 2048)
DILATIONS = (1, 4, 16)
BLOCK = 128
PAD_UNIT = max(DILATIONS) * BLOCK
SSM_WIDTH = D_MIX - ATTN_WIDTH
SSM_GROUP = 16
SSM_GROUPS = SSM_WIDTH // SSM_GROUP
SSM_STATE = 64
D_FF = 4 * D_MODEL
PROJ_WIDTH = 3 * ATTN_WIDTH + SSM_WIDTH
EPS = 1e-6
NEG_INF = -1e30
DT_MIN, DT_MAX = 1e-3, 1e-1

kernel_name = "hymba_longnet_s5_hybrid"


def _rmsnorm(x, g):
    xf = x.astype(jnp.float32)
    y = xf * lax.rsqrt(jnp.mean(xf * xf, axis=-1, keepdims=True) + EPS)
    return (y * g.astype(jnp.float32)).astype(x.dtype)


def _dilated_window(q, k, v, dilation, steps):
    b, sp, h, e = q.shape
    n = sp // dilation
    nb = n // BLOCK

    def to_blocks(t):
        t = t.reshape(b, n, dilation, h, e).transpose(0, 2, 3, 1, 4)
        return t.reshape(b, dilation, h, nb, BLOCK, e)

    qb, kb, vb = to_blocks(q), to_blocks(k), to_blocks(v)

    def with_prev(t):
        prev = jnp.pad(t[:, :, :, :-1], ((0, 0), (0, 0), (0, 0), (1, 0), (0, 0), (0, 0)))
        return jnp.concatenate([prev, t], axis=4)

    kw, vw = with_prev(kb), with_prev(vb)
    s = jnp.einsum('brhnqe,brhnke->brhnqk', qb, kw) * (HEAD_DIM ** -0.5)
    qi = jnp.arange(BLOCK)[:, None] + BLOCK
    ki = jnp.arange(2 * BLOCK)[None, :]
    dist = qi - ki
    band = (dist >= 0) & (dist <= steps)
    valid = (jnp.arange(nb)[:, None, None] * BLOCK - BLOCK + ki[None]) >= 0
    mask = band[None] & valid
    s = jnp.where(mask, s, NEG_INF)
    m = jnp.max(s, axis=-1)
    p = jnp.exp(s - m[..., None])
    l = jnp.sum(p, axis=-1)
    acc = jnp.einsum('brhnqk,brhnke->brhnqe', p, vw)

    def to_seq(t):
        t = t.reshape((b, dilation, h, n) + t.shape[5:])
        t = jnp.moveaxis(t, 3, 1)
        return t.reshape((b, sp, h) + t.shape[4:])

    return to_seq(acc), to_seq(m), to_seq(l)


def _dilated_attention(q, k, v):
    b, s, h, e = q.shape
    pad = (-s) % PAD_UNIT
    qf, kf, vf = [jnp.pad(t.astype(jnp.float32), ((0, 0), (0, pad), (0, 0), (0, 0))) for t in (q, k, v)]
    accs, ms, ls = [], [], []
    for w, d in zip(WINDOWS, DILATIONS):
        acc, m, l = _dilated_window(qf, kf, vf, d, w // d)
        accs.append(acc)
        ms.append(m)
        ls.append(l)
    m_all = jnp.stack(ms)
    wts = jnp.exp(m_all - jnp.max(m_all, axis=0, keepdims=True))
    num = jnp.sum(jnp.stack(accs) * wts[..., None], axis=0)
    den = jnp.sum(jnp.stack(ls) * wts, axis=0)
    out = num / den[..., None]
    return out[:, :s].astype(v.dtype)


def _complex_linear_combine(e1, e2):
    a1r, a1i, b1r, b1i = e1
    a2r, a2i, b2r, b2i = e2
    return (a2r * a1r - a2i * a1i,
            a2r * a1i + a2i * a1r,
            a2r * b1r - a2i * b1i + b2r,
            a2r * b1i + a2i * b1r + b2i)


def _s5_mixer(u, a_re, a_im, log_dt, b_re, b_im, c_re, c_im, d_skip, glu_w, glu_b):
    b, s, _ = u.shape
    f32 = jnp.float32
    uf = u.astype(f32).reshape(b, s, SSM_GROUPS, SSM_GROUP)
    lr, li = a_re.astype(f32), a_im.astype(f32)
    dt = jnp.exp(log_dt.astype(f32))[:, None]
    mag = jnp.exp(lr * dt)
    ab_r, ab_i = mag * jnp.cos(li * dt), mag * jnp.sin(li * dt)
    den = lr * lr + li * li
    nr, ni = ab_r - 1.0, ab_i
    cr = (nr * lr + ni * li) / den
    ci = (ni * lr - nr * li) / den
    br, bi = b_re.astype(f32), b_im.astype(f32)
    bb_r = cr[..., None] * br - ci[..., None] * bi
    bb_i = cr[..., None] * bi + ci[..., None] * br
    bu_r = jnp.einsum('bsgc,gpc->bsgp', uf, bb_r)
    bu_i = jnp.einsum('bsgc,gpc->bsgp', uf, bb_i)
    a_r = jnp.broadcast_to(ab_r, bu_r.shape)
    a_i = jnp.broadcast_to(ab_i, bu_i.shape)
    _, _, xr, xi = lax.associative_scan(_complex_linear_combine, (a_r, a_i, bu_r, bu_i), axis=1)
    y = (jnp.einsum('bsgp,gcp->bsgc', xr, c_re.astype(f32))
         - jnp.einsum('bsgp,gcp->bsgc', xi, c_im.astype(f32))
         + d_skip.astype(f32) * uf)
    y = y.reshape(b, s, SSM_WIDTH)
    z = jax.nn.gelu(y)
    out = z * jax.nn.sigmoid(z @ glu_w.astype(f32) + glu_b.astype(f32))
    return out.astype(u.dtype)


def _hybrid_layer(x, norm1_g, w_in, q_norm_g, k_norm_g, ssm_a_re, ssm_a_im, ssm_log_dt,
                  ssm_b_re, ssm_b_im, ssm_c_re, ssm_c_im, ssm_d, glu_w, glu_b,
                  attn_out_norm_g, ssm_out_norm_g, w_out, norm2_g, w_mlp_up, w_mlp_down):
    b, s, _ = x.shape
    xn = _rmsnorm(x, norm1_g)
    proj = xn @ w_in
    q = proj[..., :ATTN_WIDTH].reshape(b, s, N_HEADS, HEAD_DIM)
    k = proj[..., ATTN_WIDTH:2 * ATTN_WIDTH].reshape(b, s, N_HEADS, HEAD_DIM)
    v = proj[..., 2 * ATTN_WIDTH:3 * ATTN_WIDTH].reshape(b, s, N_HEADS, HEAD_DIM)
    u = proj[..., 3 * ATTN_WIDTH:]
    q = _rmsnorm(q, q_norm_g)
    k = _rmsnorm(k, k_norm_g)
    attn = _dilated_attention(q, k, v).reshape(b, s, ATTN_WIDTH)
    ssm = _s5_mixer(u, ssm_a_re, ssm_a_im, ssm_log_dt, ssm_b_re, ssm_b_im,
                    ssm_c_re, ssm_c_im, ssm_d, glu_w, glu_b)
    mix = jnp.concatenate([_rmsnorm(attn, attn_out_norm_g), _rmsnorm(ssm, ssm_out_norm_g)], axis=-1)
    x = x + mix @ w_out
    hdn = jnp.square(jax.nn.relu(_rmsnorm(x, norm2_g) @ w_mlp_up))
    return x + hdn @ w_mlp_down


def setup_inputs(seed: int = 0) -> dict:
    key = jax.random.key(seed)
    ks = jax.random.split(key, 20)
    L, G, P, C = DEPTH, SSM_GROUPS, SSM_STATE, SSM_GROUP
    nrm = lambda k, shape, scale: jax.random.normal(k, shape, jnp.float32) * scale
    x = nrm(ks[0], (BATCH, SEQ, D_MODEL), 1.0)
    norm1_g = 1.0 + nrm(ks[1], (L, D_MODEL), 0.02)
    w_in = nrm(ks[2], (L, D_MODEL, PROJ_WIDTH), D_MODEL ** -0.5)
    q_norm_g = 1.0 + nrm(ks[3], (L, HEAD_DIM), 0.02)
    k_norm_g = 1.0 + nrm(ks[4], (L, HEAD_DIM), 0.02)
    ssm_a_re = -0.5 + nrm(ks[5], (L, G, P), 0.01)
    ssm_a_im = math.pi * jnp.arange(P, dtype=jnp.float32)[None, None, :] + nrm(ks[6], (L, G, P), 0.01)
    ssm_log_dt = jax.random.uniform(ks[7], (L, G), jnp.float32, math.log(DT_MIN), math.log(DT_MAX))
    ssm_b_re = nrm(ks[8], (L, G, P, C), (2 * C) ** -0.5)
    ssm_b_im = nrm(ks[9], (L, G, P, C), (2 * C) ** -0.5)
    ssm_c_re = nrm(ks[10], (L, G, C, P), (2 * P) ** -0.5)
    ssm_c_im = nrm(ks[11], (L, G, C, P), (2 * P) ** -0.5)
    ssm_d = nrm(ks[12], (L, G, C), 1.0)
    glu_w = nrm(ks[13], (L, SSM_WIDTH, SSM_WIDTH), SSM_WIDTH ** -0.5)
    glu_b = nrm(ks[14], (L, SSM_WIDTH), 0.01)
    attn_out_norm_g = 1.0 + nrm(ks[15], (L, ATTN_WIDTH), 0.02)
    ssm_out_norm_g = 1.0 + nrm(ks[16], (L, SSM_WIDTH), 0.02)
    w_out = nrm(ks[17], (L, D_MIX, D_MODEL), D_MIX ** -0.5)
    norm2_g = 1.0 + nrm(ks[18], (L, D_MODEL), 0.02)
    k_up, k_down = jax.random.split(ks[19])
    w_mlp_up = nrm(k_up, (L, D_MODEL, D_FF), D_MODEL ** -0.5)
    w_mlp_down = nrm(k_down, (L, D_FF, D_MODEL), D_FF ** -0.5)
    return {"x": x, "norm1_g": norm1_g, "w_in": w_in, "q_norm_g": q_norm_g, "k_norm_g": k_norm_g,
            "ssm_a_re": ssm_a_re, "ssm_a_im": ssm_a_im, "ssm_log_dt": ssm_log_dt,
            "ssm_b_re": ssm_b_re, "ssm_b_im": ssm_b_im, "ssm_c_re": ssm_c_re, "ssm_c_im": ssm_c_im,
            "ssm_d": ssm_d, "glu_w": glu_w, "glu_b": glu_b,
            "attn_out_norm_g": attn_out_norm_g, "ssm_out_norm_g": ssm_out_norm_g,
            "w_out": w_out, "norm2_g": norm2_g, "w_mlp_up": w_mlp_up, "w_mlp_down": w_mlp_down}


def reference(x, norm1_g, w_in, q_norm_g, k_norm_g, ssm_a_re, ssm_a_im, ssm_log_dt,
              ssm_b_re, ssm_b_im, ssm_c_re, ssm_c_im, ssm_d, glu_w, glu_b,
              attn_out_norm_g, ssm_out_norm_g, w_out, norm2_g, w_mlp_up, w_mlp_down):
    h = x
    for l in range(DEPTH):
        h = _hybrid_layer(h, norm1_g[l], w_in[l], q_norm_g[l], k_norm_g[l], ssm_a_re[l], ssm_a_im[l],
                          ssm_log_dt[l], ssm_b_re[l], ssm_b_im[l], ssm_c_re[l], ssm_c_im[l], ssm_d[l],
                          glu_w[l], glu_b[l], attn_out_norm_g[l], ssm_out_norm_g[l], w_out[l],
                          norm2_g[l], w_mlp_up[l], w_mlp_down[l])
    return h
```

```python
import math
from contextlib import ExitStack
import numpy as np
import concourse.bass as bass
import concourse.mybir as mybir
from concourse.bass_utils import run_bass_kernel_spmd

F32 = mybir.dt.float32
BF16 = mybir.dt.bfloat16
AF = mybir.ActivationFunctionType
ALU = mybir.AluOpType
NCORES = 8
NT = 2048
EPS = 1e-6
NEG = -30000.0
DEBUG = False
TREE_ENG = "pool"
SKEW = 2
NB_ = 3


class Buf:
    def __init__(self, t, init=()):
        self.t = t
        self.w = {}
        self.r = {}
        self.init = list(init)

    def __getitem__(self, k):
        return self.t[k]

    def writes(self, k):
        if k is None:
            return list(self.w.values()) + self.init
        return [x for x in (self.w.get(k), self.w.get(None)) if x is not None] + self.init

    def all_tokens(self):
        out = list(self.w.values()) + self.init
        for d in self.r.values():
            out += list(d.values())
        return out

    def reads(self, k):
        out = []
        ks = list(self.r.keys()) if k is None else [k, None]
        for kk in ks:
            out += list(self.r.get(kk, {}).values())
        return out

    def set_write(self, k, tok):
        if k is None:
            self.w = {None: tok}
            self.r = {}
        else:
            self.w[k] = tok
            self.r[k] = {}

    def add_read(self, k, tok):
        d = self.r.setdefault(k, {})
        sid = id(tok[0])
        if sid not in d or d[sid][1] < tok[1]:
            d[sid] = tok


class Sched:
    ENG = ["pe", "act", "dve", "pool", "sp"]

    def __init__(self, nc, es):
        self.nc = nc
        self.ops = {e: [] for e in self.ENG}
        self.sem = {e: es.enter_context(nc.semaphore("c_" + e)) for e in self.ENG}
        self.cnt = {e: 0 for e in self.ENG}
        self.waited = {e: {} for e in self.ENG}
        self.dsems = [es.enter_context(nc.semaphore("d%d" % i)) for i in range(24)]
        self.dval = [0] * 24
        self.dnext = 0
        self.dnext_pool = 0
        self.ccsem = es.enter_context(nc.semaphore("ccs"))
        self.freed = {}
        self.all_waits = []

    def retire(self, buf):
        for sem, val in buf.all_tokens():
            sid = id(sem)
            if sid not in self.freed or self.freed[sid][1] < val:
                self.freed[sid] = (sem, val)

    @staticmethod
    def _norm(lst):
        return [(x, None) if isinstance(x, Buf) else x for x in lst]

    def _waits(self, eng, R, W, extra=()):
        deps = list(extra)
        for b, k in R:
            deps += b.writes(k)
        for b, k in W:
            deps += b.writes(k) + b.reads(k)
        need = {}
        for sem, val in deps:
            if eng == "pe" and sem is self.sem["pe"]:
                continue
            sid = id(sem)
            if self.waited[eng].get(sid, 0) >= val:
                continue
            if sid not in need or need[sid][1] < val:
                need[sid] = (sem, val)
        for sid, (sem, val) in need.items():
            self.waited[eng][sid] = val
        return list(need.values())

    def op(self, eng, fn, R=(), W=()):
        R = self._norm(R)
        W = self._norm(W)
        waits = self._waits(eng, R, W)
        self.cnt[eng] += 1
        n = self.cnt[eng]
        tok = (self.sem[eng], n)
        sem = self.sem[eng]
        self.all_waits.append(waits)

        def emit(e, waits=waits, fn=fn, sem=sem, eng=eng, n=n):
            for s, v in waits:
                e.wait_ge(s, self.xlat(s, v))
            ins = fn(e)
            if n in self.needed[eng]:
                ins.then_inc(sem, 1)

        self.ops[eng].append(emit)
        for b, k in R:
            b.add_read(k, tok)
        for b, k in W:
            b.set_write(k, tok)
        return tok

    def xlat(self, sem, val):
        eng = self.sem_eng.get(id(sem))
        if eng is None:
            return val
        import bisect
        return bisect.bisect_right(self.needed_sorted[eng], val)

    def prepare(self):
        self.sem_eng = {id(s): e for e, s in self.sem.items()}
        need = {e: set() for e in self.ENG}
        for waits in self.all_waits:
            for s, v in waits:
                e = self.sem_eng.get(id(s))
                if e is not None:
                    need[e].add(v)
        self.needed = need
        self.needed_sorted = {e: sorted(v) for e, v in need.items()}

    def dma(self, out, in_, R=(), W=(), q="sp", **kw):
        R = self._norm(R)
        W = self._norm(W)
        if q == "pool":
            i = 16 + self.dnext_pool
            self.dnext_pool = (self.dnext_pool + 1) % 8
        else:
            i = self.dnext
            self.dnext = (self.dnext + 1) % 16
        extra = [(self.dsems[i], self.dval[i])] if self.dval[i] else []
        waits = self._waits(q, R, W, extra)
        self.all_waits.append(waits)
        self.dval[i] += 16
        tok = (self.dsems[i], self.dval[i])
        ds = self.dsems[i]

        def emit(e, waits=waits, ds=ds):
            for s, v in waits:
                e.wait_ge(s, self.xlat(s, v))
            e.dma_start(out=out, in_=in_, **kw).then_inc(ds, 16)

        self.ops[q].append(emit)
        for b, k in R:
            b.add_read(k, tok)
        for b, k in W:
            b.set_write(k, tok)
        return tok

    def raw(self, eng, fn, R=(), W=(), tok=None):
        R = self._norm(R)
        W = self._norm(W)
        waits = self._waits(eng, R, W)

        def emit(e, waits=waits):
            for s, v in waits:
                e.wait_ge(s, v)
            fn(e)

        self.ops[eng].append(emit)
        for b, k in R:
            b.add_read(k, tok)
        for b, k in W:
            b.set_write(k, tok)

    def final_wait(self, eng, toks):
        self.all_waits.append(list(toks))

        def emit(e):
            for s, v in toks:
                e.wait_ge(s, self.xlat(s, v))
        self.ops[eng].append(emit)

    def run(self):
        self.prepare()
        with self.nc.Block() as block:
            @block.tensor
            def _(e):
                for f in self.ops["pe"]:
                    f(e)

            @block.scalar
            def _(e):
                for f in self.ops["act"]:
                    f(e)

            @block.vector
            def _(e):
                for f in self.ops["dve"]:
                    f(e)

            @block.gpsimd
            def _(e):
                for f in self.ops["pool"]:
                    f(e)

            @block.sync
            def _(e):
                for f in self.ops["sp"]:
                    f(e)


def blk_ap(ap2048, d, r, b):
    return ap2048.rearrange("q (n d) -> q n d", d=d)[:, 128 * b:128 * b + 128, r]


def acc_ap(acc2048, d, g):
    if d == 1:
        return acc2048[:, 512 * g:512 * g + 512].rearrange("q (b i) -> q b i", b=4)
    if d == 4:
        return acc2048.rearrange("q (b i r) -> q b r i", b=4, i=128, r=4)[:, g]
    return acc2048.rearrange("q (i r) -> q r i", r=16)[:, 4 * g:4 * g + 4]


def ps_ap(ps512, d):
    return ps512.rearrange("q (b i) -> q b i", b=4)


def qblocks(d, g):
    if d == 16:
        return [(4 * g + i, 0) for i in range(4)]
    if d == 4:
        return [(i, g) for i in range(4)]
    return [(0, 4 * g + i) for i in range(4)]


def build_nc():
    nc = bass.Bass("TRN2", target_bir_lowering=False)

    def din(name, shape, dt=F32):
        return nc.dram_tensor(name, list(shape), dt, kind="ExternalInput").ap()

    xm = din("xm", [NT, 1024])
    xh = din("xh", [NT, 1024])
    xp = din("xp", [7, NT, 1024])
    w_in = din("w_in", [1024, 2048])
    w_out = din("w_out", [1024, 1024])
    w_up = din("w_up", [1024, 4096])
    w_down = din("w_down", [4096, 1024])
    glu_w = din("glu_w", [512, 512])
    g1 = din("g1", [128, 8])
    g2 = din("g2", [128, 8])
    gao = din("gao", [128, 4])
    gso = din("gso", [128, 4])
    glub = din("glub", [128, 4])
    gq = din("gq", [128, 1])
    gk = din("gk", [128, 1])
    p_lr = din("p_lr", [128, 32])
    p_li = din("p_li", [128, 32])
    p_ldt = din("p_ldt", [128, 32])
    p_bs = din("p_bs", [128, 32, 16])
    p_bw = din("p_bw", [128, 32, 16])
    p_cs = din("p_cs", [128, 32, 16])
    p_cw = din("p_cw", [128, 32, 16])
    p_d = din("p_d", [128, 512])
    c_sgn = din("c_sgn", [128, 2])
    c_ident = din("c_ident", [128, 128])
    c_tmask = din("c_tmask", [128, 128])
    c_maskA = din("c_maskA", [128, 256])
    c_maskH = din("c_maskH", [128, 256])
    c_selA = din("c_selA", [128, 128])
    c_selB = din("c_selB", [128, 128])
    out = nc.dram_tensor("out", [NT, 1024], F32, kind="ExternalOutput").ap()
    x1d = nc.dram_tensor("x1d", [NT, 1024], F32, kind="Internal").ap()
    ssmd = nc.dram_tensor("ssmd", [128, 8192], BF16, kind="Internal").ap()
    attd = nc.dram_tensor("attd", [128, 8192], BF16, kind="Internal").ap()
    dbg = {}
    if DEBUG:
        dbg["u"] = nc.dram_tensor("dbg_u", [128, 8192], F32, kind="ExternalOutput").ap()
        dbg["ssm"] = nc.dram_tensor("dbg_ssm", [128, 8192], F32, kind="ExternalOutput").ap()
        dbg["att"] = nc.dram_tensor("dbg_att", [128, 8192], F32, kind="ExternalOutput").ap()
        dbg["hfin"] = nc.dram_tensor("dbg_hfin", [128, 64], F32, kind="ExternalOutput").ap()

    xm_t = xm.rearrange("(p j) d -> p j d", j=16)
    xh_t = xh.rearrange("(p j) d -> p j d", j=16)
    xp_t = xp.rearrange("i (p j) d -> i p j d", j=16)
    out_t = out.rearrange("(p j) d -> p j d", j=16)
    x1_t = x1d.rearrange("(p j) d -> p j d", j=16)

    with ExitStack() as es:
        S = Sched(nc, es)

        uid = [0]

        def sb(name, shape, dt, stack=es):
            uid[0] += 1
            b = Buf(stack.enter_context(nc.sbuf_tensor("%s_%d" % (name, uid[0]), list(shape), dt)),
                    init=list(S.freed.values()))
            stack.callback(lambda: S.retire(b))
            return b

        psF = [Buf(es.enter_context(nc.psum_tensor("psF%d" % i, [128, 512], F32))) for i in range(6)]
        psB = [Buf(es.enter_context(nc.psum_tensor("psB%d" % i, [128, 1024], BF16))) for i in range(2)]
        pfi = [0]
        pbi = [0]

        def nextF():
            pfi[0] = (pfi[0] + 1) % 6
            return psF[pfi[0]]

        def nextB():
            pbi[0] = (pbi[0] + 1) % 2
            return psB[pbi[0]]

        ssmdb = Buf(None)
        attdb = Buf(None)
        def dbg_dump(name, srcbuf, chunk_fn, stack):
            st = sb("dbgst", [128, 2048], F32, stack)
            for m in range(4):
                S.op("dve", lambda e, m=m: e.tensor_copy(out=st[:], in_=chunk_fn(m)), R=[srcbuf], W=[st])
                S.dma(dbg[name][:, 2048 * m:2048 * m + 2048], st[:], R=[st])

        ident_f = sb("ident_f", [128, 128], F32)
        ident = sb("ident", [128, 128], BF16)
        sgn = sb("sgn", [128, 2], F32)
        g1s = sb("g1s", [128, 8], F32)
        g2s = sb("g2s", [128, 8], F32)
        gaos = sb("gaos", [128, 4], F32)
        gsos = sb("gsos", [128, 4], F32)
        glubs = sb("glubs", [128, 4], F32)
        gqs = sb("gqs", [128, 1], F32)
        gks = sb("gks", [128, 1], F32)
        gqk = sb("gqk", [128, 1], F32)
        ones_bf = sb("ones_bf", [128, 128], BF16)
        epsb = sb("epsb", [128, 1], F32)
        S.op("pool", lambda e: e.memset(epsb[:], EPS), W=[epsb])
        S.dma(ident_f[:], c_ident, W=[ident_f])
        S.dma(sgn[:], c_sgn, W=[sgn])
        S.dma(g1s[:], g1, W=[g1s])
        S.dma(g2s[:], g2, W=[g2s])
        S.dma(gaos[:], gao, W=[gaos])
        S.dma(gsos[:], gso, W=[gsos])
        S.dma(glubs[:], glub, W=[glubs])
        S.dma(gqs[:], gq, W=[gqs])
        S.dma(gks[:], gk, W=[gks])
        S.op("dve", lambda e: e.tensor_copy(out=ident[:], in_=ident_f[:]), R=[ident_f], W=[ident])
        S.op("pool", lambda e: e.memset(ones_bf[:], 1.0), W=[ones_bf])
        S.op("dve", lambda e: e.tensor_tensor(out=gqk[:], in0=gqs[:], in1=gks[:], op=ALU.mult),
             R=[gqs, gks], W=[gqk])

        def load_w(dst_buf, dst_ap_fn, src_rows_ap, ncols, scale_buf=None, scale_col=None, key=None, eng="pool"):
            S.dma(dst_ap_fn(), src_rows_ap, W=[(dst_buf, key)], q="pool")

        ssq_pool = [sb("ssq%d" % i, [128, 4], F32) for i in range(4)]
        ssq_i = [0]

        def rmsnorm_rows(xt, xkey, xn_bf, rstd_eng="dve"):
            ssq = ssq_pool[ssq_i[0] % 4]
            ssq_i[0] += 1
            S.op("act", lambda e: e.activation(out=xn_bf[:], in_=xt[:], func=AF.Square, accum_out=ssq[:, 0:1]),
                 R=[xt], W=[xn_bf, ssq])
            S.op("act", lambda e: e.activation(out=ssq[:, 1:2], in_=ssq[:, 0:1], func=AF.Sqrt, scale=1.0 / 1024, bias=epsb[:, 0:1]),
                 R=[ssq, epsb], W=[ssq])
            S.op("dve", lambda e: e.reciprocal(out=ssq[:, 2:3], in_=ssq[:, 1:2]), R=[ssq], W=[ssq])
            S.op("act", lambda e: e.activation(out=xn_bf[:], in_=xt[:], func=AF.Copy, scale=ssq[:, 2:3]),
                 R=[xt, ssq], W=[xn_bf])

        def rms_stats(xt, xn_bf):
            ssq = ssq_pool[ssq_i[0] % 4]
            ssq_i[0] += 1
            S.op("act", lambda e: e.activation(out=xn_bf[:], in_=xt[:], func=AF.Square, accum_out=ssq[:, 0:1]),
                 R=[xt], W=[xn_bf, ssq])
            S.op("act", lambda e: e.activation(out=ssq[:, 1:2], in_=ssq[:, 0:1], func=AF.Sqrt, scale=1.0 / 1024, bias=epsb[:, 0:1]),
                 R=[ssq, epsb], W=[ssq])
            S.op("dve", lambda e: e.reciprocal(out=ssq[:, 2:3], in_=ssq[:, 1:2]), R=[ssq], W=[ssq])
            return ssq

        def rms_apply(xt, xn_bf, ssq):
            S.op("act", lambda e: e.activation(out=xn_bf[:], in_=xt[:], func=AF.Copy, scale=ssq[:, 2:3]),
                 R=[xt, ssq], W=[xn_bf])

        def transpose8(xn_bf, dst_fn, dstbuf, dkey, gbuf):
            pb = nextB()
            for dc in range(8):
                S.op("pe", lambda e, dc=dc: e.transpose(out=pb[:, dc * 128:(dc + 1) * 128],
                                                         in_=xn_bf[:, dc * 128:(dc + 1) * 128], identity=ident[:]),
                     R=[xn_bf, ident], W=[(pb, dc)])
            S.op("dve", lambda e: e.tensor_tensor(out=dst_fn(), in0=pb[:].rearrange("q (c p) -> q c p", c=8),
                                                  in1=gbuf[:].unsqueeze(2).broadcast_to([128, 8, 128]), op=ALU.mult),
                 R=[pb, gbuf], W=[(dstbuf, dkey)])

        def fm_rmsnorm(srcT, dstT, gbuf, nchunk, es2, dbuf):
            sqs = [sb("fm_sq", [128, 2048], BF16, es2), sb("fm_sq2", [128, 2048], BF16, es2)]
            rstd = sb("fm_rstd", [128, 2048], F32, es2)
            pss = [nextF() for _ in range(4)]
            for m in range(nchunk):
                sq = sqs[m % 2]
                S.op("act", lambda e, m=m, sq=sq: e.activation(out=sq[:], in_=srcT[:, m, :], func=AF.Square),
                     R=[(srcT, m)], W=[sq])
                for n in range(4):
                    S.op("pe", lambda e, m=m, n=n, sq=sq: e.matmul(pss[n][:], lhsT=ones_bf[:], rhs=sq[:, 512 * n:512 * n + 512],
                                                                    start=(m == 0), stop=(m == nchunk - 1)),
                         R=[sq, ones_bf], W=[pss[n]])
            for n in range(4):
                S.op("act", lambda e, n=n: e.activation(out=rstd[:, 512 * n:512 * n + 512], in_=pss[n][:], func=AF.Ln,
                                                        scale=1.0 / (128 * nchunk), bias=epsb[:, 0:1]),
                     R=[pss[n], epsb], W=[(rstd, n)])
                S.op("act", lambda e, n=n: e.activation(out=rstd[:, 512 * n:512 * n + 512], in_=rstd[:, 512 * n:512 * n + 512],
                                                        func=AF.Exp, scale=-0.5),
                     R=[(rstd, n)], W=[(rstd, n)])
            for m in range(nchunk):
                S.op("dve", lambda e, m=m: e.scalar_tensor_tensor(out=dstT(m), in0=srcT[:, m, :],
                                                                  scalar=gbuf[:, m:m + 1], in1=rstd[:],
                                                                  op0=ALU.mult, op1=ALU.mult),
                     R=[(srcT, m), rstd, gbuf], W=[(dbuf, m)])


        with ExitStack() as esA:
            u_bf = sb("u_bf", [128, 32, 16, 16], BF16, esA)
            UT = sb("UT", [128, 32, 2, 128], BF16, esA)
            Tm = sb("Tm", [128, 32, 2, 128], BF16, esA)
            Em = sb("Em", [128, 32, 17, 16], BF16, esA)
            dsk = sb("dsk", [128, 512], F32, esA)
            S.dma(dsk[:], p_d, W=[dsk])
            Hbf = sb("Hbf", [128, 32, 128], BF16, esA)
            wu = sb("wu", [128, 8, 512], BF16, esA)
            for dc in range(8):
                load_w(wu, lambda dc=dc: wu[:, dc, :], w_in[dc * 128:(dc + 1) * 128, 1536:2048], 512, g1s, dc, key=dc)
            V = "dve"
            with ExitStack() as e2:
                LV1 = sb("LV1", [128, 8, 2, 32], F32, e2)
                LV2 = sb("LV2", [128, 8, 2, 32], F32, e2)
                c1 = sb("c1", [128, 2, 32], F32, e2)
                c2 = sb("c2", [128, 2, 32], F32, e2)
                pc1 = sb("pc1", [128, 2, 32], F32, e2)
                pc2 = sb("pc2", [128, 2, 32], F32, e2)
                Hacc = sb("Hacc", [128, 2, 32], F32, e2)
                Fm = sb("Fm", [128, 32, 2, 128], BF16, e2)
                with ExitStack() as eb:
                    def pt(name, shape=(128, 32)):
                        return sb(name, list(shape), F32, eb)

                    def tt(o, a, b, op, eng=V):
                        S.op(eng, lambda e: e.tensor_tensor(out=o[:], in0=a[:], in1=b[:], op=op), R=[a, b], W=[o])

                    def tsc(o, a, s1, s2, op0, op1=None, eng=V):
                        if op1 is None:
                            S.op(eng, lambda e: e.tensor_scalar(out=o[:], in0=a[:], scalar1=s1, scalar2=None, op0=op0), R=[a], W=[o])
                        else:
                            S.op(eng, lambda e: e.tensor_scalar(out=o[:], in0=a[:], scalar1=s1, scalar2=s2, op0=op0, op1=op1), R=[a], W=[o])
                    lr, li, ldt = pt("lr"), pt("li"), pt("ldt")
                    BS, BW, CS, CW = pt("BS", (128, 32, 16)), pt("BW", (128, 32, 16)), pt("CS", (128, 32, 16)), pt("CW", (128, 32, 16))
                    for bfr, src in ((lr, p_lr), (li, p_li), (ldt, p_ldt), (BS, p_bs), (BW, p_bw), (CS, p_cs), (CW, p_cw)):
                        S.dma(bfr[:], src, W=[bfr])
                    dtv, mag, th, t1, t2, sn, cs_, ar, ai = [pt("pp%d" % i) for i in range(9)]
                    S.op("act", lambda e: e.activation(out=dtv[:], in_=ldt[:], func=AF.Exp), R=[ldt], W=[dtv])
                    tt(t1, lr, dtv, ALU.mult)
                    S.op("act", lambda e: e.activation(out=mag[:], in_=t1[:], func=AF.Exp), R=[t1], W=[mag])
                    tt(th, li, dtv, ALU.mult)
                    def reduce_pi(r):
                        for _ in range(5):
                            tsc(t3r, r, math.pi, 2 * math.pi, ALU.is_gt, ALU.mult)
                            tt(r, r, t3r, ALU.subtract)
                    t3r = pt("t3r")
                    tsc(t1, th, 0.0, None, ALU.add)
                    reduce_pi(t1)
                    S.op("act", lambda e: e.activation(out=sn[:], in_=t1[:], func=AF.Sin), R=[t1], W=[sn])
                    tsc(t2, th, 0.5 * math.pi, None, ALU.add)
                    reduce_pi(t2)
                    S.op("act", lambda e: e.activation(out=cs_[:], in_=t2[:], func=AF.Sin), R=[t2], W=[cs_])
                    tt(ar, mag, cs_, ALU.mult)
                    tt(ai, mag, sn, ALU.mult)
                    den, nr, cr, ci, t3 = [pt("pq%d" % i) for i in range(5)]
                    tt(den, lr, lr, ALU.mult)
                    tt(t1, li, li, ALU.mult)
                    tt(den, den, t1, ALU.add)
                    S.op(V, lambda e: e.reciprocal(out=den[:], in_=den[:]), R=[den], W=[den])
                    tsc(nr, ar, -1.0, None, ALU.add)
                    tt(t1, nr, lr, ALU.mult)
                    tt(t2, ai, li, ALU.mult)
                    tt(cr, t1, t2, ALU.add)
                    tt(cr, cr, den, ALU.mult)
                    tt(t1, ai, lr, ALU.mult)
                    tt(t2, nr, li, ALU.mult)
                    tt(ci, t1, t2, ALU.subtract)
                    tt(ci, ci, den, ALU.mult)
                    PWr, PWi = pt("PWr", (128, 32, 17)), pt("PWi", (128, 32, 17))
                    S.op(V, lambda e: e.memset(PWr[:, :, 0], 1.0), W=[PWr])
                    S.op(V, lambda e: e.memset(PWi[:, :, 0], 0.0), W=[PWi])
                    S.op(V, lambda e: e.tensor_copy(out=PWr[:, :, 1], in_=ar[:]), R=[ar], W=[PWr])
                    S.op(V, lambda e: e.tensor_copy(out=PWi[:, :, 1], in_=ai[:]), R=[ai], W=[PWi])
                    dA, dB = pt("dA", (128, 32, 8)), pt("dB", (128, 32, 8))
                    for n_ in (1, 2, 4, 8):
                        def bcr(n_=n_):
                            return PWr[:, :, n_].unsqueeze(2).broadcast_to([128, 32, n_])

                        def bci(n_=n_):
                            return PWi[:, :, n_].unsqueeze(2).broadcast_to([128, 32, n_])
                        S.op(V, lambda e, n_=n_, bcr=bcr: e.tensor_tensor(out=dA[:, :, 0:n_], in0=PWr[:, :, 1:n_ + 1], in1=bcr(), op=ALU.mult), R=[PWr], W=[dA])
                        S.op(V, lambda e, n_=n_, bci=bci: e.tensor_tensor(out=dB[:, :, 0:n_], in0=PWi[:, :, 1:n_ + 1], in1=bci(), op=ALU.mult), R=[PWi], W=[dB])
                        S.op(V, lambda e, n_=n_: e.tensor_tensor(out=PWr[:, :, n_ + 1:2 * n_ + 1], in0=dA[:, :, 0:n_], in1=dB[:, :, 0:n_], op=ALU.subtract), R=[dA, dB], W=[PWr])
                        S.op(V, lambda e, n_=n_, bci=bci: e.tensor_tensor(out=dA[:, :, 0:n_], in0=PWr[:, :, 1:n_ + 1], in1=bci(), op=ALU.mult), R=[PWr, PWi], W=[dA])
                        S.op(V, lambda e, n_=n_, bcr=bcr: e.tensor_tensor(out=dB[:, :, 0:n_], in0=PWi[:, :, 1:n_ + 1], in1=bcr(), op=ALU.mult), R=[PWi, PWr], W=[dB])
                        S.op(V, lambda e, n_=n_: e.tensor_tensor(out=PWi[:, :, n_ + 1:2 * n_ + 1], in0=dA[:, :, 0:n_], in1=dB[:, :, 0:n_], op=ALU.add), R=[dA, dB], W=[PWi])
                    Nr, Ni, m2 = pt("Nr", (128, 32, 16)), pt("Ni", (128, 32, 16)), pt("m2", (128, 32, 16))
                    m3 = pt("m3", (128, 32, 16))
                    S.op(V, lambda e: e.tensor_tensor(out=m2[:], in0=PWr[:, :, 0:16], in1=PWr[:, :, 0:16], op=ALU.mult), R=[PWr], W=[m2])
                    S.op(V, lambda e: e.tensor_tensor(out=m3[:], in0=PWi[:, :, 0:16], in1=PWi[:, :, 0:16], op=ALU.mult), R=[PWi], W=[m3])
                    tt(m2, m2, m3, ALU.add)
                    S.op(V, lambda e: e.reciprocal(out=m2[:], in_=m2[:]), R=[m2], W=[m2])
                    S.op(V, lambda e: e.tensor_tensor(out=Nr[:], in0=PWr[:, :, 0:16], in1=m2[:], op=ALU.mult), R=[PWr, m2], W=[Nr])
                    S.op(V, lambda e: e.tensor_tensor(out=Ni[:], in0=PWi[:, :, 0:16], in1=m2[:], op=ALU.mult), R=[PWi, m2], W=[Ni])
                    tsc(Ni, Ni, -1.0, None, ALU.mult)
                    a16r, a16is, a15r, a15is, ars, ais = [pt("pa%d" % i) for i in range(6)]
                    S.op(V, lambda e: e.tensor_copy(out=a16r[:], in_=PWr[:, :, 16]), R=[PWr], W=[a16r])
                    S.op(V, lambda e: e.tensor_scalar(out=a16is[:], in0=PWi[:, :, 16], scalar1=sgn[:, 0:1], scalar2=None, op0=ALU.mult), R=[PWi, sgn], W=[a16is])
                    S.op(V, lambda e: e.tensor_copy(out=a15r[:], in_=PWr[:, :, 15]), R=[PWr], W=[a15r])
                    S.op(V, lambda e: e.tensor_scalar(out=a15is[:], in0=PWi[:, :, 15], scalar1=sgn[:, 0:1], scalar2=None, op0=ALU.mult), R=[PWi, sgn], W=[a15is])
                    BbS, sBW, tb1, tb2 = pt("BbS", (128, 32, 16)), pt("sBW", (128, 32, 16)), pt("tb1", (128, 32, 16)), pt("tb2", (128, 32, 16))
                    sCS = pt("sCS", (128, 32, 16))
                    S.op(V, lambda e: e.tensor_scalar(out=sBW[:], in0=BW[:], scalar1=sgn[:, 0:1], scalar2=None, op0=ALU.mult), R=[BW, sgn], W=[sBW])
                    S.op(V, lambda e: e.tensor_scalar(out=sCS[:], in0=CS[:], scalar1=sgn[:, 1:2], scalar2=None, op0=ALU.mult), R=[CS, sgn], W=[sCS])

                    def bc16(b):
                        return b[:].unsqueeze(2).broadcast_to([128, 32, 16])
                    S.op(V, lambda e: e.tensor_tensor(out=tb1[:], in0=BS[:], in1=bc16(cr), op=ALU.mult), R=[BS, cr], W=[tb1])
                    S.op(V, lambda e: e.tensor_tensor(out=tb2[:], in0=sBW[:], in1=bc16(ci), op=ALU.mult), R=[sBW, ci], W=[tb2])
                    tt(BbS, tb1, tb2, ALU.add)

                    BbW = sb("BbW", [128, 32, 16], F32, eb)
                    S.dma(BbW[0:64], BbS[64:128], R=[BbS], W=[(BbW, 0)])
                    S.dma(BbW[64:128], BbS[0:64], R=[BbS], W=[(BbW, 1)])
                    sBbW = sb("sBbW", [128, 32, 16], F32, eb)
                    S.op(V, lambda e: e.tensor_scalar(out=sBbW[:], in0=BbW[:], scalar1=sgn[:, 0:1], scalar2=None, op0=ALU.mult), R=[BbW, sgn], W=[sBbW])
                    tmk = sb("tmask", [128, 128], F32, eb)
                    S.dma(tmk[:], c_tmask, W=[tmk])
                    btemps = []
                    for bi in range(2):
                        btemps.append((sb("Wa", [128, 4, 17, 16], F32, eb), sb("Wb", [128, 4, 17, 16], F32, eb),
                                       sb("Ya", [128, 4, 16, 16], F32, eb), sb("Yb", [128, 4, 16, 16], F32, eb),
                                       sb("Yw", [128, 4, 16, 16], F32, eb), sb("Ybf", [128, 4, 16, 16], BF16, eb),
                                       sb("Wbf", [128, 4, 16, 16], BF16, eb), sb("FTb", [128, 4, 16, 16], BF16, eb)))

                    def batch(qd):
                        gs = slice(4 * qd, 4 * qd + 4)
                        if True:
                            Wa, Wb, Ya, Yb, Yw, Ybf, Wbf, FTb = btemps[qd % 2]

                            def bk(b, n):
                                return b[:, gs, 0:n].unsqueeze(3).broadcast_to([128, 4, n, 16])

                            def bcm(b, n):
                                return b[:, gs, :].unsqueeze(2).broadcast_to([128, 4, n, 16])
                            S.op(V, lambda e: e.tensor_tensor(out=Wa[:], in0=bk(PWr, 17), in1=bcm(sCS, 17), op=ALU.mult), R=[PWr, sCS], W=[Wa])
                            S.op("pool", lambda e: e.tensor_tensor(out=Wb[:], in0=bk(PWi, 17), in1=bcm(CW, 17), op=ALU.mult), R=[PWi, CW], W=[Wb])
                            S.op(V, lambda e: e.tensor_tensor(out=Em[:, gs], in0=Wa[:], in1=Wb[:], op=ALU.subtract), R=[Wa, Wb], W=[(Em, qd)])
                            S.op(V, lambda e: e.tensor_tensor(out=Ya[:], in0=bk(Nr, 16), in1=bcm(BbS, 16), op=ALU.mult), R=[Nr, BbS], W=[Ya])
                            S.op("pool", lambda e: e.tensor_tensor(out=Yb[:], in0=bk(Ni, 16), in1=bcm(sBbW, 16), op=ALU.mult), R=[Ni, sBbW], W=[Yb])
                            S.op(V, lambda e: e.tensor_tensor(out=Ya[:], in0=Ya[:], in1=Yb[:], op=ALU.add), R=[Ya, Yb], W=[Ya])
                            S.op("act", lambda e: e.activation(out=Ybf[:], in_=Ya[:], func=AF.Copy), R=[Ya], W=[Ybf])
                            S.op("act", lambda e: e.activation(out=Wbf[:], in_=Em[:, gs, 0:16, :], func=AF.Copy), R=[(Em, qd)], W=[Wbf])
                            S.dma(Yw[0:64], Ya[64:128], R=[Ya], W=[(Yw, 0)])
                            S.dma(Yw[64:128], Ya[0:64], R=[Ya], W=[(Yw, 1)])
                            def b15(b):
                                return b[:, gs].unsqueeze(2).unsqueeze(3).broadcast_to([128, 4, 16, 16])
                            S.op(V, lambda e: e.tensor_tensor(out=Yb[:], in0=Ya[:], in1=b15(a15r), op=ALU.mult), R=[Ya, a15r], W=[Yb])
                            S.op("pool", lambda e: e.tensor_tensor(out=Yw[:], in0=Yw[:], in1=b15(a15is), op=ALU.mult), R=[Yw, a15is], W=[Yw])
                            S.op(V, lambda e: e.tensor_tensor(out=FTb[:], in0=Yb[:], in1=Yw[:], op=ALU.add), R=[Yb, Yw], W=[FTb])
                            for gl in range(4):
                                g = 4 * qd + gl
                                pb = nextB()
                                for J in range(2):
                                    S.op("pe", lambda e, gl=gl, J=J, pb=pb: e.transpose(
                                        out=pb[:, 128 * J:128 * J + 128],
                                        in_=FTb[:, gl, 8 * J:8 * J + 8, :].rearrange("q j c -> q (j c)"), identity=ident[:]),
                                         R=[FTb, ident], W=[(pb, J)])
                                S.op("dve", lambda e, g=g, pb=pb: e.tensor_copy(
                                    out=Fm[:, g, :, :].rearrange("q J k -> q (J k)"), in_=pb[:, 0:256]),
                                     R=[pb], W=[(Fm, g)])
                                pf = nextF()
                                for tb in range(2):
                                    S.op("pe", lambda e, gl=gl, tb=tb, pf=pf: e.matmul(
                                        pf[:, 128 * tb:128 * tb + 128],
                                        lhsT=Ybf[:, gl, 0:8, :].rearrange("q j c -> q (j c)"),
                                        rhs=Wbf[:, gl, 8 * tb:8 * tb + 8, :].rearrange("q j c -> q (j c)"),
                                        start=True, stop=True), R=[Ybf, Wbf], W=[(pf, tb)])
                                S.op("dve", lambda e, g=g, pf=pf: e.tensor_tensor(out=Tm[:, g, 0, :], in0=pf[:, 0:128], in1=tmk[:], op=ALU.mult),
                                     R=[pf, tmk], W=[(Tm, g)])
                                S.op("act", lambda e, g=g, pf=pf: e.activation(out=Tm[:, g, 1, :], in_=pf[:, 128:256], func=AF.Copy),
                                     R=[pf], W=[(Tm, g)])
                    for qd in range(8):
                        batch(qd)
                    qr, qi, q1, q2 = pt("qr"), pt("qi"), pt("q1"), pt("q2")
                    S.op(V, lambda e: e.tensor_copy(out=qr[:], in_=PWr[:, :, 16]), R=[PWr], W=[qr])
                    S.op(V, lambda e: e.tensor_copy(out=qi[:], in_=PWi[:, :, 16]), R=[PWi], W=[qi])
                    for L in range(8):
                        S.op(V, lambda e, L=L: e.tensor_copy(out=LV1[:, L, 0, :], in_=qr[:]), R=[qr], W=[LV1])
                        S.op(V, lambda e, L=L: e.tensor_copy(out=LV1[:, L, 1, :], in_=qr[:]), R=[qr], W=[LV1])
                        S.op(V, lambda e, L=L: e.tensor_scalar(out=LV2[:, L, 0, :], in0=qi[:], scalar1=sgn[:, 0:1], scalar2=None, op0=ALU.mult), R=[qi, sgn], W=[LV2])
                        S.op(V, lambda e, L=L: e.tensor_scalar(out=LV2[:, L, 1, :], in0=qi[:], scalar1=sgn[:, 1:2], scalar2=None, op0=ALU.mult), R=[qi, sgn], W=[LV2])
                        if L < 7:
                            tt(q1, qr, qr, ALU.mult)
                            tt(q2, qi, qi, ALU.mult)
                            tt(qi, qr, qi, ALU.mult)
                            tsc(qi, qi, 2.0, None, ALU.mult)
                            tt(qr, q1, q2, ALU.subtract)

                def cmul_acc(dst, src, L, add, n=None, eng=None):
                    if n is None:
                        k1, k2a, k2b = LV1[:, L], LV2[:, L, 0, :], LV2[:, L, 1, :]
                        cc1, cc2 = (pc1, pc2) if eng == 'pool' else (c1, c2)
                        t1_, t2_ = cc1[:], cc2[:]
                        t2a, t2b = cc2[:, 0, :], cc2[:, 1, :]
                        sa, sb_ = (lambda: src()[:, 1, :]), (lambda: src()[:, 0, :])
                    else:
                        k1 = LV1[:, L].unsqueeze(1).broadcast_to([128, n, 2, 32])
                        k2a = LV2[:, L, 0, :].unsqueeze(1).broadcast_to([128, n, 32])
                        k2b = LV2[:, L, 1, :].unsqueeze(1).broadcast_to([128, n, 32])
                        t1_, t2_ = tc1[:, 0:n], tc2[:, 0:n]
                        t2a, t2b = tc2[:, 0:n, 0, :], tc2[:, 0:n, 1, :]
                        sa, sb_ = (lambda: src()[:, :, 1, :]), (lambda: src()[:, :, 0, :])
                    tb1_, tb2_ = (((pc1, pc2) if eng == 'pool' else (c1, c2)) if n is None else (tc1, tc2))
                    S.op(eng or V, lambda e: e.tensor_tensor(out=t1_, in0=src(), in1=k1, op=ALU.mult), R=[srcbuf[0], LV1], W=[tb1_])
                    S.op(eng or V, lambda e: e.tensor_tensor(out=t2a, in0=sa(), in1=k2a, op=ALU.mult), R=[srcbuf[0], LV2], W=[tb2_])
                    S.op(eng or V, lambda e: e.tensor_tensor(out=t2b, in0=sb_(), in1=k2b, op=ALU.mult), R=[srcbuf[0], LV2], W=[tb2_])
                    if add is not None:
                        S.op(eng or V, lambda e: e.tensor_tensor(out=t1_, in0=t1_, in1=add(), op=ALU.add), R=[tb1_, srcbuf[1]], W=[tb1_])
                    S.op(eng or V, lambda e: e.tensor_tensor(out=dst(), in0=t1_, in1=t2_, op=ALU.add), R=[tb1_, tb2_], W=[srcbuf[2]])

                srcbuf = [None, None, None]
                SS = sb("SS", [128, 2, 128, 32], F32, e2)
                with ExitStack() as ep:
                    TT0 = sb("TT0", [128, 64, 2, 32], F32, ep)
                    tc1 = sb("tc1", [128, 32, 2, 32], F32, ep)
                    tc2 = sb("tc2", [128, 32, 2, 32], F32, ep)
                    TT1 = sb("TT1", [128, 32, 2, 32], F32, ep)
                    xts = [sb("xtA%d" % i, [128, 1024], F32, ep) for i in range(NB_)]
                    xnb = [sb("xnbA%d" % i, [128, 1024], BF16, ep) for i in range(NB_)]
                    xTt = [sb("xTtA%d" % i, [128, 8, 128], BF16, ep) for i in range(2)]

                    def uproj_chunk(src_t):
                        sq_of = {}
                        pus = {}
                        for j in range(16 + 4):
                            if j < 16:
                                xt, xn = xts[j % NB_], xnb[j % NB_]
                                S.dma(xt[:], src_t[:, j, :], W=[xt])
                                sq_of[j] = rms_stats(xt, xn)
                            j1 = j - 1
                            if 0 <= j1 < 16:
                                xt, xn, xT = xts[j1 % NB_], xnb[j1 % NB_], xTt[j1 % 2]
                                rms_apply(xt, xn, sq_of[j1])
                                transpose8(xn, lambda xT=xT: xT[:], xT, None, g1s)
                            jj = j - 2
                            if 0 <= jj < 16:
                                xT2 = xTt[jj % 2]
                                pu = nextF()
                                pus[jj] = pu
                                for dc in range(8):
                                    S.op("pe", lambda e, dc=dc, xT2=xT2, pu=pu: e.matmul(pu[:], lhsT=xT2[:, dc, :], rhs=wu[:, dc, :],
                                                                                     start=(dc == 0), stop=(dc == 7)),
                                         R=[xT2, (wu, dc)], W=[pu])
                            j3 = j - 3
                            if 0 <= j3 < 16:
                                pu3 = pus[j3]
                                S.op("act", lambda e, j3=j3, pu3=pu3: e.activation(
                                    out=u_bf[:, :, j3, :], in_=pu3[:].rearrange("q (g c) -> q g c", g=32), func=AF.Copy),
                                     R=[pu3], W=[(u_bf, j3)])
                        for g in range(32):
                            if g % 4 == 0:
                                pb = nextB()
                            for J in range(2):
                                o = ((g % 4) * 2 + J) * 128
                                S.op("pe", lambda e, g=g, J=J, o=o, pb=pb: e.transpose(
                                    out=pb[:, o:o + 128],
                                    in_=u_bf[:, g, 8 * J:8 * J + 8, :].rearrange("q j c -> q (j c)"), identity=ident[:]),
                                     R=[u_bf, ident], W=[(pb, (g % 4) * 2 + J)])
                            if g % 4 == 3:
                                g0 = g - 3
                                S.op("dve", lambda e, g0=g0, pb=pb: e.tensor_copy(
                                    out=UT[:, g0:g0 + 4, :, :].rearrange("q g J k -> q (g J k)"), in_=pb[:]),
                                     R=[pb], W=[(UT, g0 // 4)])

                    def s1_chunk():
                        for g in range(32):
                            if g % 4 == 0:
                                pf = nextF()
                            o = (g % 4) * 128
                            for J in range(2):
                                S.op("pe", lambda e, g=g, J=J, o=o, pf=pf: e.matmul(pf[:, o:o + 128], lhsT=Fm[:, g, J, :], rhs=UT[:, g, J, :],
                                                                                 start=(J == 0), stop=(J == 1)),
                                     R=[(Fm, g), (UT, g // 4)], W=[(pf, g % 4)])
                            if g % 4 == 3:
                                g0 = g - 3
                                S.op("dve", lambda e, g0=g0, pf=pf: e.tensor_copy(
                                    out=SS[:, 0, :, g0:g0 + 4].rearrange("q k g -> q g k"),
                                    in_=pf[:].rearrange("q (g k) -> q g k", g=4)), R=[pf], W=[(SS, ("s", g0))])
                        S.dma(SS[0:64, 1, :, :], SS[64:128, 0, :, :], R=[SS], W=[(SS, "w0")])
                        S.dma(SS[64:128, 1, :, :], SS[0:64, 0, :, :], R=[SS], W=[(SS, "w1")])

                    def kview(buf, n):
                        if buf is SS:
                            return SS[:, :, 0:n, :].rearrange("q s k g -> q k s g")
                        return buf[:, 0:n]

                    def tree():
                        bufs = [TT0, TT1]
                        n = 128
                        for L in range(7):
                            src_b = SS if L == 0 else bufs[(L + 1) % 2]
                            dst_b = bufs[L % 2]
                            no = n // 2
                            for h0 in range(0, no, 32):
                                hn = min(32, no - h0)
                                ev = lambda src_b=src_b, n=n, h0=h0, hn=hn: kview(src_b, n).rearrange("q (m two) s g -> q m two s g", two=2)[:, h0:h0 + hn, 0]
                                od = lambda src_b=src_b, n=n, h0=h0, hn=hn: kview(src_b, n).rearrange("q (m two) s g -> q m two s g", two=2)[:, h0:h0 + hn, 1]
                                ds = lambda dst_b=dst_b, h0=h0, hn=hn: dst_b[:, h0:h0 + hn]
                                srcbuf[0], srcbuf[1], srcbuf[2] = src_b, src_b, dst_b
                                cmul_acc(ds, ev, L, od, n=hn, eng=TREE_ENG)
                            n = no
                        return bufs[6 % 2]

                    S.op(V, lambda e: e.memset(Hacc[:], 0.0), W=[Hacc])
                    for i in range(7):
                        uproj_chunk(xp_t[i])
                        s1_chunk()
                        rb = tree()
                        srcbuf[0], srcbuf[1], srcbuf[2] = Hacc, rb, Hacc
                        cmul_acc(lambda: Hacc[:], lambda: Hacc[:], 7, lambda rb=rb: rb[:, 0], eng=TREE_ENG)
                    uproj_chunk(xm_t)
                    s1_chunk()
                XX = sb("XX", [128, 129, 2, 32], F32, e2)
                S.op(V, lambda e: e.tensor_copy(out=XX[:, 0], in_=Hacc[:]), R=[Hacc], W=[XX])
                for k in range(128):
                    srcbuf[0], srcbuf[1], srcbuf[2] = XX, SS, XX
                    cmul_acc(lambda k=k: XX[:, k + 1], lambda k=k: XX[:, k], 0, lambda k=k: SS[:, :, k, :])
                S.op(V, lambda e: e.tensor_copy(out=Hbf[:], in_=XX[:, 0:128, 0, :].rearrange("q k g -> q g k")), R=[XX], W=[Hbf])
                if DEBUG:
                    S.dma(dbg["hfin"][:, 0:32], XX[:, 128, 0, :], R=[XX])
                    S.dma(dbg["hfin"][:, 32:64], Hacc[:, 0, :], R=[Hacc])

            ztm = sb("ztm", [128, 16, 512], BF16, esA)
            with ExitStack() as e4:
                yb = [sb("yb%d" % i, [128, 2, 256], F32, e4) for i in range(3)]
                g2b = [sb("g2b%d" % i, [128, 2, 256], F32, e4) for i in range(3)]
                for gp in range(16):
                    pf = nextF()
                    y, t = yb[gp % 3], g2b[gp % 3]
                    for gl in range(2):
                        g = 2 * gp + gl
                        o = 256 * gl
                        S.op("pe", lambda e, g=g, o=o, pf=pf: e.matmul(pf[:, o:o + 128], lhsT=UT[:, g, 0, :], rhs=Tm[:, g, 0, :], start=True, stop=False),
                             R=[(UT, g // 4), (Tm, g)], W=[(pf, gl)])
                        S.op("pe", lambda e, g=g, o=o, pf=pf: e.matmul(pf[:, o:o + 128], lhsT=Hbf[:, g, :],
                                                                       rhs=Em[:, g, 1:9, :].rearrange("q t c -> q (t c)"), start=False, stop=True),
                             R=[Hbf, (Em, g // 4)], W=[(pf, gl)])
                        S.op("pe", lambda e, g=g, o=o, pf=pf: e.matmul(pf[:, o + 128:o + 256], lhsT=UT[:, g, 0, :], rhs=Tm[:, g, 1, :], start=True, stop=False),
                             R=[(UT, g // 4), (Tm, g)], W=[(pf, gl)])
                        S.op("pe", lambda e, g=g, o=o, pf=pf: e.matmul(pf[:, o + 128:o + 256], lhsT=UT[:, g, 1, :], rhs=Tm[:, g, 0, :], start=False, stop=False),
                             R=[(UT, g // 4), (Tm, g)], W=[(pf, gl)])
                        S.op("pe", lambda e, g=g, o=o, pf=pf: e.matmul(pf[:, o + 128:o + 256], lhsT=Hbf[:, g, :],
                                                                       rhs=Em[:, g, 9:17, :].rearrange("q t c -> q (t c)"), start=False, stop=True),
                             R=[Hbf, (Em, g // 4)], W=[(pf, gl)])
                    ug = ztm[:, :, 32 * gp:32 * gp + 32].rearrange("q j (g c) -> q g j c", g=2)
                    dg = dsk[:, 32 * gp:32 * gp + 32].rearrange("q (g c) -> q g c", g=2).unsqueeze(2).broadcast_to([128, 2, 16, 16])
                    S.op("dve", lambda e, gp=gp, y=y, dg=dg: e.tensor_tensor(
                        out=y[:].rearrange("q g (j c) -> q g j c", j=16), in0=u_bf[:, 2 * gp:2 * gp + 2], in1=dg, op=ALU.mult),
                         R=[u_bf, dsk], W=[y])
                    S.op("dve", lambda e, y=y, pf=pf: e.tensor_tensor(out=y[:], in0=y[:], in1=pf[:].rearrange("q (g x) -> q g x", g=2), op=ALU.add),
                         R=[y, pf], W=[y])
                    S.op("pool", lambda e, y=y, t=t: e.tensor_tensor(out=t[:], in0=y[:], in1=y[:], op=ALU.mult), R=[y], W=[t])
                    S.op("pool", lambda e, t=t: e.tensor_scalar(out=t[:], in0=t[:], scalar1=0.044715, scalar2=1.0, op0=ALU.mult, op1=ALU.add), R=[t], W=[t])
                    S.op("pool", lambda e, y=y, t=t: e.tensor_tensor(out=t[:], in0=t[:], in1=y[:], op=ALU.mult), R=[t, y], W=[t])
                    S.op("act", lambda e, t=t: e.activation(out=t[:], in_=t[:], func=AF.Sigmoid, scale=1.5957691216), R=[t], W=[t])
                    S.op("dve", lambda e, y=y, t=t, ug=ug: e.tensor_tensor(
                        out=ug, in0=y[:].rearrange("q g (j c) -> q g j c", j=16), in1=t[:].rearrange("q g (j c) -> q g j c", j=16), op=ALU.mult),
                         R=[y, t], W=[(ztm, gp)])
                if DEBUG:
                    dbg_dump("u", ztm, lambda m: ztm[:, 4 * m:4 * m + 4, :].rearrange("q j x -> q (j x)"), e4)
            zT = sb("zT", [128, 4, 2048], BF16, esA)
            for j in range(16):
                pb = nextB()
                for m in range(4):
                    S.op("pe", lambda e, j=j, m=m, pb=pb: e.transpose(
                        out=pb[:, 128 * m:128 * m + 128],
                        in_=ztm[:, j, 128 * m:128 * m + 128], identity=ident[:]),
                         R=[ztm, ident], W=[(pb, m)])
                S.op("dve", lambda e, j=j, pb=pb: e.tensor_copy(out=zT[:, :, 128 * j:128 * j + 128],
                                                                 in_=pb[:, 0:512].rearrange("q (m p) -> q m p", m=4)),
                     R=[pb], W=[(zT, j // 4)])
            with ExitStack() as e5:
                wg = sb("wg", [128, 4, 512], BF16, e5)
                for kc in range(4):
                    load_w(wg, lambda kc=kc: wg[:, kc, :], glu_w[kc * 128:(kc + 1) * 128, :], 512, key=kc)
                sT = sb("sT", [128, 4, 2048], F32, e5)
                sg = [sb("sg%d" % i, [128, 512], F32, e5) for i in range(2)]
                for n in range(4):
                    for m in range(4):
                        pf = nextF()
                        for kc in range(4):
                            S.op("pe", lambda e, n=n, m=m, kc=kc, pf=pf: e.matmul(
                                pf[:], lhsT=wg[:, kc, 128 * m:128 * m + 128], rhs=zT[:, kc, 512 * n:512 * n + 512],
                                start=(kc == 0), stop=(kc == 3)), R=[(wg, kc), (zT, n)], W=[pf])
                        s_ = sg[(n * 4 + m) % 2]
                        S.op("act", lambda e, m=m, pf=pf, s_=s_: e.activation(out=s_[:], in_=pf[:], func=AF.Sigmoid,
                                                                             bias=glubs[:, m:m + 1]), R=[pf, glubs], W=[s_])
                        S.op("dve", lambda e, n=n, m=m, s_=s_: e.tensor_tensor(out=sT[:, m, 512 * n:512 * n + 512],
                                                                              in0=zT[:, m, 512 * n:512 * n + 512], in1=s_[:], op=ALU.mult),
                             R=[s_, (zT, n)], W=[(sT, m)])
                ssmN = sb("ssmN", [128, 4, 2048], BF16, e5)
                fm_rmsnorm(sT, lambda m: ssmN[:, m, :], gsos, 4, e5, ssmN)
                S.dma(ssmd, ssmN[:].rearrange("q m f -> q (m f)"), R=[ssmN], W=[ssmdb])
                if DEBUG:
                    dbg_dump("ssm", ssmN, lambda m: ssmN[:, m, :], e5)

        with ExitStack() as esB:
            xnT = sb("xnT", [128, 8, 4096], BF16, esB)
            attT = sb("attT", [128, 4, 2048], F32, esB)
            attN = sb("attN", [128, 4, 2048], BF16, esB)
            VA = sb("VA", [128, 69, 128], BF16, esB)
            VB = sb("VB", [128, 69, 128], BF16, esB)
            S.op("pool", lambda e: e.memset(VA[:, :, 64:128], 0.0), W=[VA])
            S.op("pool", lambda e: e.memset(VB[:, :, 0:64], 0.0), W=[VB])
            maskA = sb("maskA", [128, 256], BF16, esB)
            maskH = sb("maskH", [128, 256], BF16, esB)
            selA = sb("selA", [128, 128], BF16, esB)
            selB = sb("selB", [128, 128], BF16, esB)
            with ExitStack() as e1:
                mt = sb("mt", [128, 256], F32, e1)
                for dstb, src, n in ((maskA, c_maskA, 256), (maskH, c_maskH, 256), (selA, c_selA, 128), (selB, c_selB, 128)):
                    S.dma(mt[:, :n], src, W=[mt])
                    S.op("dve", lambda e, dstb=dstb, n=n: e.tensor_copy(out=dstb[:], in_=mt[:, :n]), R=[mt], W=[dstb])
                xts = [sb("xtB%d" % i, [128, 1024], F32, e1) for i in range(3)]
                xnb = [sb("xnbB%d" % i, [128, 1024], BF16, e1) for i in range(3)]
                tl = [(hm, src, j) for hm, src in ((0, xh_t), (1, xm_t)) for j in range(16)]
                sq_of = {}
                for n in range(len(tl) + 1):
                    if n < len(tl):
                        hm, src, j = tl[n]
                        S.dma(xts[n % 3][:], src[:, j, :], W=[xts[n % 3]])
                        sq_of[n] = rms_stats(xts[n % 3], xnb[n % 3])
                    if n >= 1:
                        hm, src, j = tl[n - 1]
                        xt, xn = xts[(n - 1) % 3], xnb[(n - 1) % 3]
                        rms_apply(xt, xn, sq_of[n - 1])
                        transpose8(xn, lambda hm=hm, j=j: xnT[:, :, 2048 * hm:2048 * hm + 2048].rearrange(
                            "q c (p j) -> q c p j", j=16)[:, :, :, j], xnT, (hm, j // 4), g1s)

            with ExitStack() as e2:
                wq = sb("wq", [128, 8, 128], BF16, e2)
                wk = sb("wk", [128, 8, 128], BF16, e2)
                wv = sb("wv", [128, 8, 128], BF16, e2)
                qT = sb("qT", [128, 2048], BF16, e2)
                kT = sb("kT", [128, 4096], BF16, e2)
                accN = sb("accN", [128, 2048], F32, e2)
                accD = sb("accD", [128, 2048], F32, e2)
                sqb = sb("sqb", [128, 512], BF16, e2)
                rs = sb("rs", [128, 512], F32, e2)
                PT = [sb("PT%d" % i, [128, 512], BF16, e2) for i in range(3)]
                blockones = sb("blockones", [128, 128], BF16, e2)
                S.op("pool", lambda e: e.memset(blockones[:], 0.0), W=[blockones])
                S.op("pool", lambda e: e.memset(blockones[0:64, 0:64], 1.0), W=[blockones])
                S.op("pool", lambda e: e.memset(blockones[64:128, 64:128], 1.0), W=[blockones])

                vblocks = []
                vidx = {}
                for d in (1, 4, 16):
                    nres = 1 if d == 1 else d
                    nb = 16 // nres if d != 16 else 1
                    nb = {1: 16, 4: 4, 16: 1}[d]
                    for r in range(nres):
                        vidx[(d, r, -1)] = len(vblocks)
                        vblocks.append((0, d, r, nb - 1))
                        for b in range(nb):
                            vidx[(d, r, b)] = len(vblocks)
                            vblocks.append((2048, d, r, b))
                assert len(vblocks) == 69

                for hp in range(4):
                    for dc in range(8):
                        rows = w_in[dc * 128:(dc + 1) * 128, :]
                        load_w(wq, lambda dc=dc: wq[:, dc, :], rows[:, 128 * hp:128 * hp + 128], 128, g1s, dc, key=dc)
                        load_w(wk, lambda dc=dc: wk[:, dc, :], rows[:, 512 + 128 * hp:512 + 128 * hp + 128], 128, g1s, dc, key=dc)
                        load_w(wv, lambda dc=dc: wv[:, dc, :], rows[:, 1024 + 128 * hp:1024 + 128 * hp + 128], 128, g1s, dc, key=dc)

                    def qk_proj(wb, dst, f_src0, f_dst0, gain):
                        for n in range(4):
                            pf = nextF()
                            for dc in range(8):
                                S.op("pe", lambda e, dc=dc, pf=pf, n=n: e.matmul(
                                    pf[:], lhsT=wb[:, dc, :], rhs=xnT[:, dc, f_src0 + 512 * n:f_src0 + 512 * n + 512],
                                    start=(dc == 0), stop=(dc == 7)), R=[(wb, dc), xnT], W=[pf])
                            S.op("act", lambda e, pf=pf: e.activation(out=sqb[:], in_=pf[:], func=AF.Square), R=[pf], W=[sqb])
                            p2 = nextF()
                            S.op("pe", lambda e, p2=p2: e.matmul(p2[:], lhsT=blockones[:], rhs=sqb[:], start=True, stop=True),
                                 R=[blockones, sqb], W=[p2])
                            S.op("act", lambda e, p2=p2: e.activation(out=rs[:], in_=p2[:], func=AF.Ln, scale=1.0 / 64, bias=epsb[:, 0:1]),
                                 R=[p2, epsb], W=[rs])
                            S.op("act", lambda e: e.activation(out=rs[:], in_=rs[:], func=AF.Exp, scale=-0.5), R=[rs], W=[rs])
                            o = f_dst0 + 512 * n
                            if gain is None:
                                S.op("dve", lambda e, pf=pf, o=o: e.tensor_tensor(out=dst[:, o:o + 512], in0=pf[:], in1=rs[:], op=ALU.mult),
                                     R=[pf, rs], W=[(dst, o // 512)])
                            else:
                                S.op("dve", lambda e, pf=pf, o=o: e.scalar_tensor_tensor(
                                    out=dst[:, o:o + 512], in0=pf[:], scalar=gain[:, 0:1], in1=rs[:], op0=ALU.mult, op1=ALU.mult),
                                     R=[pf, rs, gain], W=[(dst, o // 512)])
                    qk_proj(wq, qT, 2048, 0, gqk)
                    qk_proj(wk, kT, 0, 0, None)
                    qk_proj(wk, kT, 2048, 2048, None)
                    for vi, (fb, d, r, b) in enumerate(vblocks):
                        pf = nextF()
                        for dc in range(8):
                            S.op("pe", lambda e, dc=dc, pf=pf, fb=fb, d=d, r=r, b=b: e.matmul(
                                pf[:, 0:128], lhsT=blk_ap(xnT[:, dc, fb:fb + 2048], d, r, b), rhs=wv[:, dc, :],
                                start=(dc == 0), stop=(dc == 7)), R=[xnT, (wv, dc)], W=[pf])
                        S.op("act", lambda e, vi=vi, pf=pf: e.activation(out=VA[:, vi, 0:64], in_=pf[:, 0:64], func=AF.Copy),
                             R=[pf], W=[(VA, vi)])
                        S.op("dve", lambda e, vi=vi, pf=pf: e.tensor_copy(out=VB[:, vi, 64:128], in_=pf[:, 64:128]),
                             R=[pf], W=[(VB, vi)])
                    qlist = [(d, g, qix, r, b) for d in (1, 4, 16) for g in range(4) for qix, (r, b) in enumerate(qblocks(d, g))]

                    def stage_S(n):
                        d, g, qix, r, b = qlist[n]
                        halo = (b == 0)
                        ps = psF[4 + n % 2]
                        pt_ = PT[n % 3]
                        mk = maskH if halo else maskA
                        for h in range(2):
                            rs_ = slice(64 * h, 64 * h + 64)
                            o = 256 * h
                            S.op("pe", lambda e, o=o, ps=ps, mk=mk: e.matmul(ps[:, o:o + 256], lhsT=ident[:], rhs=mk[:], start=True, stop=False),
                                 R=[ident, mk], W=[(ps, h)])
                            if halo:
                                nbk = {1: 16, 4: 4, 16: 1}[d]
                                kprev = blk_ap(kT[rs_, 0:2048], d, r, nbk - 1)
                            else:
                                kprev = blk_ap(kT[rs_, 2048:4096], d, r, b - 1)
                            kcur = blk_ap(kT[rs_, 2048:4096], d, r, b)
                            qa = blk_ap(qT[rs_, :], d, r, b)
                            S.op("pe", lambda e, o=o, ps=ps, kprev=kprev, qa=qa: e.matmul(ps[:, o:o + 128], lhsT=kprev, rhs=qa, start=False, stop=False),
                                 R=[kT, qT], W=[(ps, h)])
                            S.op("pe", lambda e, o=o, ps=ps, kcur=kcur, qa=qa: e.matmul(ps[:, o + 128:o + 256], lhsT=kcur, rhs=qa, start=False, stop=True),
                                 R=[kT, qT], W=[(ps, h)])
                        S.op("act", lambda e, ps=ps, pt_=pt_: e.activation(out=pt_[:], in_=ps[:], func=AF.Exp, scale=0.125),
                             R=[ps], W=[pt_])

                    def stage_PV(n):
                        d, g, qix, r, b = qlist[n]
                        grp = n // 4
                        pn = psF[grp % 2]
                        pd = psF[2 + grp % 2]
                        pt_ = PT[n % 3]
                        vp = vidx[(d, r, b - 1)]
                        vc = vidx[(d, r, b)]
                        oq = 128 * qix
                        seq = [(VA, vp, 0), (VA, vc, 128), (VB, vp, 256), (VB, vc, 384)]
                        for si, (Vb, vi, po) in enumerate(seq):
                            S.op("pe", lambda e, Vb=Vb, vi=vi, po=po, oq=oq, pn=pn, pt_=pt_, si=si: e.matmul(
                                pn[:, oq:oq + 128], lhsT=Vb[:, vi, :], rhs=pt_[:, po:po + 128], start=(si == 0), stop=(si == 3)),
                                 R=[(Vb, vi), pt_], W=[(pn, qix)])
                        for si, (sel, po) in enumerate([(selA, 0), (selA, 128), (selB, 256), (selB, 384)]):
                            S.op("pe", lambda e, sel=sel, po=po, oq=oq, pd=pd, pt_=pt_, si=si: e.matmul(
                                pd[:, oq:oq + 128], lhsT=sel[:], rhs=pt_[:, po:po + 128], start=(si == 0), stop=(si == 3)),
                                 R=[sel, pt_], W=[(pd, qix)])
                        if qix == 3:
                            for acc, pp in ((accN, pn), (accD, pd)):
                                if d == 1:
                                    S.op("dve", lambda e, acc=acc, pp=pp, d=d, g=g: e.tensor_copy(out=acc_ap(acc[:], d, g), in_=ps_ap(pp[:], d)),
                                         R=[pp], W=[acc])
                                else:
                                    S.op("dve", lambda e, acc=acc, pp=pp, d=d, g=g: e.tensor_tensor(
                                        out=acc_ap(acc[:], d, g), in0=acc_ap(acc[:], d, g), in1=ps_ap(pp[:], d), op=ALU.add),
                                         R=[pp, acc], W=[acc])

                    for n in range(len(qlist) + 1):
                        if n < len(qlist):
                            stage_S(n)
                        if n >= 1:
                            stage_PV(n - 1)
                    S.op("act", lambda e: e.activation(out=accD[:], in_=accD[:], func=AF.Ln), R=[accD], W=[accD])
                    S.op("act", lambda e: e.activation(out=accD[:], in_=accD[:], func=AF.Exp, scale=-1.0), R=[accD], W=[accD])
                    S.op("dve", lambda e, hp=hp: e.tensor_tensor(out=attT[:, hp, :], in0=accN[:], in1=accD[:], op=ALU.mult),
                         R=[accN, accD], W=[(attT, hp)])
            with ExitStack() as e3:
                fm_rmsnorm(attT, lambda m: attN[:, m, :], gaos, 4, e3, attN)
                S.dma(attd, attN[:].rearrange("q m f -> q (m f)"), R=[attN], W=[attdb])
                if DEBUG:
                    dbg_dump("att", attN, lambda m: attN[:, m, :], e3)

        with ExitStack() as esC:
            wupb = sb("wupb", [128, 8, 4096], BF16, esC)
            wdnb = sb("wdnb", [128, 32, 1024], BF16, esC)
            x1b = Buf(None)
            with ExitStack() as e1:
                wo = sb("wo", [128, 8, 1024], BF16, e1)
                for m in range(8):
                    load_w(wo, lambda m=m: wo[:, m, :], w_out[m * 128:(m + 1) * 128, :], 1024, key=m)
                ssmN2 = sb("ssmN2", [128, 4, 2048], BF16, e1)
                S.dma(ssmN2[:].rearrange("q m f -> q (m f)"), ssmd, R=[ssmdb], W=[ssmN2])
                attN2 = sb("attN2", [128, 4, 2048], BF16, e1)
                S.dma(attN2[:].rearrange("q m f -> q (m f)"), attd, R=[attdb], W=[attN2])
                for dc in range(8):
                    for q4 in range(4):
                        load_w(wupb, lambda dc=dc, q4=q4: wupb[:, dc, 1024 * q4:1024 * q4 + 1024],
                               w_up[dc * 128:(dc + 1) * 128, 1024 * q4:1024 * q4 + 1024], 1024, g2s, dc, key=(dc, q4),
                               eng="pool")
                for fc in range(32):
                    load_w(wdnb, lambda fc=fc: wdnb[:, fc, :], w_down[fc * 128:(fc + 1) * 128, :], 1024, key=fc,
                           eng="pool")
                xts = [sb("xtC%d" % i, [128, 1024], F32, e1) for i in range(3)]
                for j in range(16):
                    xt = xts[j % 3]
                    S.dma(xt[:], xm_t[:, j, :], W=[xt])
                    for hf_ in range(2):
                        pf = nextF()
                        for m in range(8):
                            S.op("pe", lambda e, m=m, j=j, hf_=hf_, pf=pf: e.matmul(
                                pf[:], lhsT=(attN2[:, m, :].rearrange("q (p j) -> q p j", j=16)[:, :, j] if m < 4 else ssmN2[:, m - 4, 128 * j:128 * j + 128]),
                                rhs=wo[:, m, 512 * hf_:512 * hf_ + 512],
                                start=(m == 0), stop=(m == 7)), R=[attN2, ssmN2, (wo, m)], W=[pf])
                        S.op("dve", lambda e, hf_=hf_, pf=pf, xt=xt: e.tensor_tensor(
                            out=xt[:, 512 * hf_:512 * hf_ + 512], in0=xt[:, 512 * hf_:512 * hf_ + 512], in1=pf[:], op=ALU.add),
                             R=[pf, xt], W=[xt])
                    S.dma(x1_t[:, j, :], xt[:], R=[xt], W=[(x1b, j)])
            xn2T = sb("xn2T", [128, 8, 2048], BF16, esC)
            hT = sb("hT", [128, 32, 256], BF16, esC)
            rl = [sb("rl%d" % i, [128, 256], BF16, esC) for i in range(2)]
            xts = [sb("xtD%d" % i, [128, 1024], F32, esC) for i in range(3)]
            xnb = [sb("xnbD%d" % i, [128, 1024], BF16, esC) for i in range(2)]
            sq_of = {}
            for n in range(17):
                if n < 16:
                    S.dma(xts[n % 3][:], x1_t[:, n, :], R=[(x1b, n)], W=[xts[n % 3]])
                    sq_of[n] = rms_stats(xts[n % 3], xnb[n % 2])
                if n >= 1:
                    j = n - 1
                    rms_apply(xts[j % 3], xnb[j % 2], sq_of[j])
                    transpose8(xnb[j % 2], lambda j=j: xn2T[:, :, 128 * j:128 * j + 128], xn2T, j // 2, g2s)
            otoks = []
            for n in range(8):
                for fc in range(32):
                    pf = nextF()
                    for dc in range(8):
                        S.op("pe", lambda e, n=n, fc=fc, dc=dc, pf=pf: e.matmul(
                            pf[:, 0:256], lhsT=wupb[:, dc, 128 * fc:128 * fc + 128], rhs=xn2T[:, dc, 256 * n:256 * n + 256],
                            start=(dc == 0), stop=(dc == 7)), R=[(wupb, (dc, fc // 8)), (xn2T, n)], W=[pf])
                    r_ = rl[fc % 2]
                    S.op("act", lambda e, pf=pf, r_=r_: e.activation(out=r_[:], in_=pf[:, 0:256], func=AF.Relu), R=[pf], W=[r_])
                    S.op("pool", lambda e, fc=fc, r_=r_: e.tensor_tensor(out=hT[:, fc, :], in0=r_[:], in1=r_[:], op=ALU.mult),
                         R=[r_], W=[(hT, fc)])
                for jt in range(2):
                    j = 2 * n + jt
                    xt = xts[j % 2]
                    S.dma(xt[:], x1_t[:, j, :], R=[(x1b, j)], W=[xt])
                    for hf_ in range(2):
                        pf = nextF()
                        for fc in range(32):
                            S.op("pe", lambda e, fc=fc, jt=jt, hf_=hf_, pf=pf: e.matmul(
                                pf[:], lhsT=hT[:, fc, 128 * jt:128 * jt + 128], rhs=wdnb[:, fc, 512 * hf_:512 * hf_ + 512],
                                start=(fc == 0), stop=(fc == 31)), R=[(hT, fc), (wdnb, fc)], W=[pf])
                        S.op("dve", lambda e, hf_=hf_, pf=pf, xt=xt: e.tensor_tensor(
                            out=xt[:, 512 * hf_:512 * hf_ + 512], in0=xt[:, 512 * hf_:512 * hf_ + 512], in1=pf[:], op=ALU.add),
                             R=[pf, xt], W=[xt])
                    otoks.append(S.dma(out_t[:, j, :], xt[:], R=[xt]))
            S.final_wait("sp", otoks)
        S.run()
    return nc


def _stack(a, b):
    return np.ascontiguousarray(np.concatenate([a, b], axis=0).astype(np.float32))


def kernel(**inputs):
    f = lambda k: np.asarray(inputs[k], dtype=np.float32)
    x = f("x")[0]
    col = lambda v, n: np.ascontiguousarray(v.reshape(n, 128).T)
    lr = f("ssm_a_re")[0].T
    li = f("ssm_a_im")[0].T
    ldt = np.broadcast_to(f("ssm_log_dt")[0][None, :], (128, 32))
    bre = f("ssm_b_re")[0].transpose(1, 0, 2)
    bim = f("ssm_b_im")[0].transpose(1, 0, 2)
    cre = f("ssm_c_re")[0].transpose(2, 0, 1)
    cim = f("ssm_c_im")[0].transpose(2, 0, 1)
    sgn = np.zeros((128, 2), np.float32)
    sgn[:64, 0] = -1; sgn[64:, 0] = 1; sgn[:64, 1] = 1; sgn[64:, 1] = -1
    kk = np.arange(128)
    tmask = ((kk[None, :] // 16) >= (kk[:, None] // 16)).astype(np.float32)
    mprev = np.where(kk[:, None] >= kk[None, :], 0.0, NEG).astype(np.float32)
    mcur = np.where(kk[:, None] <= kk[None, :], 0.0, NEG).astype(np.float32)
    maskA = np.concatenate([mprev, mcur], axis=1)
    selA = np.zeros((128, 128), np.float32); selA[:, :64] = 1
    selB = np.zeros((128, 128), np.float32); selB[:, 64:] = 1
    common = {
        "w_in": f("w_in")[0], "w_out": f("w_out")[0], "w_up": f("w_mlp_up")[0], "w_down": f("w_mlp_down")[0],
        "glu_w": f("glu_w")[0],
        "g1": col(f("norm1_g")[0], 8), "g2": col(f("norm2_g")[0], 8),
        "gao": col(f("attn_out_norm_g")[0], 4), "gso": col(f("ssm_out_norm_g")[0], 4),
        "glub": col(f("glu_b")[0], 4),
        "gq": np.ascontiguousarray(np.tile(f("q_norm_g")[0], 2)[:, None]),
        "gk": np.ascontiguousarray(np.tile(f("k_norm_g")[0], 2)[:, None]),
        "p_lr": _stack(lr, lr), "p_li": _stack(li, li), "p_ldt": np.ascontiguousarray(ldt),
        "p_bs": _stack(bre, bim), "p_bw": _stack(bim, bre),
        "p_cs": _stack(cre, cim), "p_cw": _stack(cim, cre),
        "p_d": np.ascontiguousarray(np.broadcast_to(f("ssm_d")[0].reshape(1, 512), (128, 512))),
        "c_sgn": sgn, "c_ident": np.eye(128, dtype=np.float32), "c_tmask": tmask,
        "c_maskA": maskA, "c_selA": selA, "c_selB": selB,
    }
    in_maps = []
    for c in range(NCORES):
        m = dict(common)
        m["xm"] = np.ascontiguousarray(x[c * NT:(c + 1) * NT])
        m["xh"] = np.ascontiguousarray(x[(c - 1) * NT:c * NT]) if c > 0 else np.zeros((NT, 1024), np.float32)
        mh = maskA.copy()
        if c == 0:
            mh[:, :128] = NEG
        m["c_maskH"] = mh
        xpp = np.zeros((7, NT, 1024), np.float32)
        for i in range(7):
            cc = c - 7 + i
            if cc >= 0:
                xpp[i] = x[cc * NT:(cc + 1) * NT]
        m["xp"] = xpp
        in_maps.append(m)
    nc = build_nc()
    res = run_bass_kernel_spmd(nc, in_maps, core_ids=list(range(NCORES)))
    kernel.last = res
    out = np.concatenate([res.results[c]["out"] for c in range(NCORES)], axis=0)
    return out.reshape(1, NCORES * NT, 1024).astype(np.float32)
```

```python
import math
from contextlib import ExitStack
import numpy as np
import concourse.bass as bass
import concourse.mybir as mybir
from concourse.bass_utils import run_bass_kernel_spmd

F32 = mybir.dt.float32
BF16 = mybir.dt.bfloat16
AF = mybir.ActivationFunctionType
ALU = mybir.AluOpType
NCORES = 8
NT = 2048
EPS = 1e-6
NEG = -30000.0
DEBUG = False
TREE_ENG = "pool"
SKEW = 2
NB_ = 3


class Buf:
    def __init__(self, t, init=()):
        self.t = t
        self.w = {}
        self.r = {}
        self.init = list(init)

    def __getitem__(self, k):
        return self.t[k]

    def writes(self, k):
        if k is None:
            return list(self.w.values()) + self.init
        return [x for x in (self.w.get(k), self.w.get(None)) if x is not None] + self.init

    def all_tokens(self):
        out = list(self.w.values()) + self.init
        for d in self.r.values():
            out += list(d.values())
        return out

    def reads(self, k):
        out = []
        ks = list(self.r.keys()) if k is None else [k, None]
        for kk in ks:
            out += list(self.r.get(kk, {}).values())
        return out

    def set_write(self, k, tok):
        if k is None:
            self.w = {None: tok}
            self.r = {}
        else:
            self.w[k] = tok
            self.r[k] = {}

    def add_read(self, k, tok):
        d = self.r.setdefault(k, {})
        sid = id(tok[0])
        if sid not in d or d[sid][1] < tok[1]:
            d[sid] = tok


class Sched:
    ENG = ["pe", "act", "dve", "pool", "sp"]

    def __init__(self, nc, es):
        self.nc = nc
        self.ops = {e: [] for e in self.ENG}
        self.sem = {e: es.enter_context(nc.semaphore("c_" + e)) for e in self.ENG}
        self.cnt = {e: 0 for e in self.ENG}
        self.waited = {e: {} for e in self.ENG}
        self.dsems = [es.enter_context(nc.semaphore("d%d" % i)) for i in range(24)]
        self.dval = [0] * 24
        self.dnext = 0
        self.dnext_pool = 0
        self.ccsem = es.enter_context(nc.semaphore("ccs"))
        self.freed = {}
        self.all_waits = []

    def retire(self, buf):
        for sem, val in buf.all_tokens():
            sid = id(sem)
            if sid not in self.freed or self.freed[sid][1] < val:
                self.freed[sid] = (sem, val)

    @staticmethod
    def _norm(lst):
        return [(x, None) if isinstance(x, Buf) else x for x in lst]

    def _waits(self, eng, R, W, extra=()):
        deps = list(extra)
        for b, k in R:
            deps += b.writes(k)
        for b, k in W:
            deps += b.writes(k) + b.reads(k)
        need = {}
        for sem, val in deps:
            if eng == "pe" and sem is self.sem["pe"]:
                continue
            sid = id(sem)
            if self.waited[eng].get(sid, 0) >= val:
                continue
            if sid not in need or need[sid][1] < val:
                need[sid] = (sem, val)
        for sid, (sem, val) in need.items():
            self.waited[eng][sid] = val
        return list(need.values())

    def op(self, eng, fn, R=(), W=()):
        R = self._norm(R)
        W = self._norm(W)
        waits = self._waits(eng, R, W)
        self.cnt[eng] += 1
        n = self.cnt[eng]
        tok = (self.sem[eng], n)
        sem = self.sem[eng]
        self.all_waits.append(waits)

        def emit(e, waits=waits, fn=fn, sem=sem, eng=eng, n=n):
            for s, v in waits:
                e.wait_ge(s, self.xlat(s, v))
            ins = fn(e)
            if n in self.needed[eng]:
                ins.then_inc(sem, 1)

        self.ops[eng].append(emit)
        for b, k in R:
            b.add_read(k, tok)
        for b, k in W:
            b.set_write(k, tok)
        return tok

    def xlat(self, sem, val):
        eng = self.sem_eng.get(id(sem))
        if eng is None:
            return val
        import bisect
        return bisect.bisect_right(self.needed_sorted[eng], val)

    def prepare(self):
        self.sem_eng = {id(s): e for e, s in self.sem.items()}
        need = {e: set() for e in self.ENG}
        for waits in self.all_waits:
            for s, v in waits:
                e = self.sem_eng.get(id(s))
                if e is not None:
                    need[e].add(v)
        self.needed = need
        self.needed_sorted = {e: sorted(v) for e, v in need.items()}

    def dma(self, out, in_, R=(), W=(), q="sp", **kw):
        R = self._norm(R)
        W = self._norm(W)
        if q == "pool":
            i = 16 + self.dnext_pool
            self.dnext_pool = (self.dnext_pool + 1) % 8
        else:
            i = self.dnext
            self.dnext = (self.dnext + 1) % 16
        extra = [(self.dsems[i], self.dval[i])] if self.dval[i] else []
        waits = self._waits(q, R, W, extra)
        self.all_waits.append(waits)
        self.dval[i] += 16
        tok = (self.dsems[i], self.dval[i])
        ds = self.dsems[i]

        def emit(e, waits=waits, ds=ds):
            for s, v in waits:
                e.wait_ge(s, self.xlat(s, v))
            e.dma_start(out=out, in_=in_, **kw).then_inc(ds, 16)

        self.ops[q].append(emit)
        for b, k in R:
            b.add_read(k, tok)
        for b, k in W:
            b.set_write(k, tok)
        return tok

    def raw(self, eng, fn, R=(), W=(), tok=None):
        R = self._norm(R)
        W = self._norm(W)
        waits = self._waits(eng, R, W)

        def emit(e, waits=waits):
            for s, v in waits:
                e.wait_ge(s, v)
            fn(e)

        self.ops[eng].append(emit)
        for b, k in R:
            b.add_read(k, tok)
        for b, k in W:
            b.set_write(k, tok)

    def final_wait(self, eng, toks):
        self.all_waits.append(list(toks))

        def emit(e):
            for s, v in toks:
                e.wait_ge(s, self.xlat(s, v))
        self.ops[eng].append(emit)

    def run(self):
        self.prepare()
        with self.nc.Block() as block:
            @block.tensor
            def _(e):
                for f in self.ops["pe"]:
                    f(e)

            @block.scalar
            def _(e):
                for f in self.ops["act"]:
                    f(e)

            @block.vector
            def _(e):
                for f in self.ops["dve"]:
                    f(e)

            @block.gpsimd
            def _(e):
                for f in self.ops["pool"]:
                    f(e)

            @block.sync
            def _(e):
                for f in self.ops["sp"]:
                    f(e)


def blk_ap(ap2048, d, r, b):
    return ap2048.rearrange("q (n d) -> q n d", d=d)[:, 128 * b:128 * b + 128, r]


def acc_ap(acc2048, d, g):
    if d == 1:
        return acc2048[:, 512 * g:512 * g + 512].rearrange("q (b i) -> q b i", b=4)
    if d == 4:
        return acc2048.rearrange("q (b i r) -> q b r i", b=4, i=128, r=4)[:, g]
    return acc2048.rearrange("q (i r) -> q r i", r=16)[:, 4 * g:4 * g + 4]


def ps_ap(ps512, d):
    return ps512.rearrange("q (b i) -> q b i", b=4)


def qblocks(d, g):
    if d == 16:
        return [(4 * g + i, 0) for i in range(4)]
    if d == 4:
        return [(i, g) for i in range(4)]
    return [(0, 4 * g + i) for i in range(4)]


def build_nc():
    nc = bass.Bass("TRN2", target_bir_lowering=False)

    def din(name, shape, dt=F32):
        return nc.dram_tensor(name, list(shape), dt, kind="ExternalInput").ap()

    xm = din("xm", [NT, 1024])
    xh = din("xh", [NT, 1024])
    xp = din("xp", [7, NT, 1024])
    w_in = din("w_in", [1024, 2048])
    w_out = din("w_out", [1024, 1024])
    w_up = din("w_up", [1024, 4096])
    w_down = din("w_down", [4096, 1024])
    glu_w = din("glu_w", [512, 512])
    g1 = din("g1", [128, 8])
    g2 = din("g2", [128, 8])
    gao = din("gao", [128, 4])
    gso = din("gso", [128, 4])
    glub = din("glub", [128, 4])
    gq = din("gq", [128, 1])
    gk = din("gk", [128, 1])
    p_lr = din("p_lr", [128, 32])
    p_li = din("p_li", [128, 32])
    p_ldt = din("p_ldt", [128, 32])
    p_bs = din("p_bs", [128, 32, 16])
    p_bw = din("p_bw", [128, 32, 16])
    p_cs = din("p_cs", [128, 32, 16])
    p_cw = din("p_cw", [128, 32, 16])
    p_d = din("p_d", [128, 512])
    c_sgn = din("c_sgn", [128, 2])
    c_ident = din("c_ident", [128, 128])
    c_tmask = din("c_tmask", [128, 128])
    c_maskA = din("c_maskA", [128, 256])
    c_maskH = din("c_maskH", [128, 256])
    c_selA = din("c_selA", [128, 128])
    c_selB = din("c_selB", [128, 128])
    out = nc.dram_tensor("out", [NT, 1024], F32, kind="ExternalOutput").ap()
    x1d = nc.dram_tensor("x1d", [NT, 1024], F32, kind="Internal").ap()
    ssmd = nc.dram_tensor("ssmd", [128, 8192], BF16, kind="Internal").ap()
    attd = nc.dram_tensor("attd", [128, 8192], BF16, kind="Internal").ap()
    dbg = {}
    if DEBUG:
        dbg["u"] = nc.dram_tensor("dbg_u", [128, 8192], F32, kind="ExternalOutput").ap()
        dbg["ssm"] = nc.dram_tensor("dbg_ssm", [128, 8192], F32, kind="ExternalOutput").ap()
        dbg["att"] = nc.dram_tensor("dbg_att", [128, 8192], F32, kind="ExternalOutput").ap()
        dbg["hfin"] = nc.dram_tensor("dbg_hfin", [128, 64], F32, kind="ExternalOutput").ap()

    xm_t = xm.rearrange("(p j) d -> p j d", j=16)
    xh_t = xh.rearrange("(p j) d -> p j d", j=16)
    xp_t = xp.rearrange("i (p j) d -> i p j d", j=16)
    out_t = out.rearrange("(p j) d -> p j d", j=16)
    x1_t = x1d.rearrange("(p j) d -> p j d", j=16)

    with ExitStack() as es:
        S = Sched(nc, es)

        uid = [0]

        def sb(name, shape, dt, stack=es):
            uid[0] += 1
            b = Buf(stack.enter_context(nc.sbuf_tensor("%s_%d" % (name, uid[0]), list(shape), dt)),
                    init=list(S.freed.values()))
            stack.callback(lambda: S.retire(b))
            return b

        psF = [Buf(es.enter_context(nc.psum_tensor("psF%d" % i, [128, 512], F32))) for i in range(6)]
        psB = [Buf(es.enter_context(nc.psum_tensor("psB%d" % i, [128, 1024], BF16))) for i in range(2)]
        pfi = [0]
        pbi = [0]

        def nextF():
            pfi[0] = (pfi[0] + 1) % 6
            return psF[pfi[0]]

        def nextB():
            pbi[0] = (pbi[0] + 1) % 2
            return psB[pbi[0]]

        ssmdb = Buf(None)
        attdb = Buf(None)
        def dbg_dump(name, srcbuf, chunk_fn, stack):
            st = sb("dbgst", [128, 2048], F32, stack)
            for m in range(4):
                S.op("dve", lambda e, m=m: e.tensor_copy(out=st[:], in_=chunk_fn(m)), R=[srcbuf], W=[st])
                S.dma(dbg[name][:, 2048 * m:2048 * m + 2048], st[:], R=[st])

        ident_f = sb("ident_f", [128, 128], F32)
        ident = sb("ident", [128, 128], BF16)
        sgn = sb("sgn", [128, 2], F32)
        g1s = sb("g1s", [128, 8], F32)
        g2s = sb("g2s", [128, 8], F32)
        gaos = sb("gaos", [128, 4], F32)
        gsos = sb("gsos", [128, 4], F32)
        glubs = sb("glubs", [128, 4], F32)
        gqs = sb("gqs", [128, 1], F32)
        gks = sb("gks", [128, 1], F32)
        gqk = sb("gqk", [128, 1], F32)
        ones_bf = sb("ones_bf", [128, 128], BF16)
        epsb = sb("epsb", [128, 1], F32)
        S.op("pool", lambda e: e.memset(epsb[:], EPS), W=[epsb])
        S.dma(ident_f[:], c_ident, W=[ident_f])
        S.dma(sgn[:], c_sgn, W=[sgn])
        S.dma(g1s[:], g1, W=[g1s])
        S.dma(g2s[:], g2, W=[g2s])
        S.dma(gaos[:], gao, W=[gaos])
        S.dma(gsos[:], gso, W=[gsos])
        S.dma(glubs[:], glub, W=[glubs])
        S.dma(gqs[:], gq, W=[gqs])
        S.dma(gks[:], gk, W=[gks])
        S.op("dve", lambda e: e.tensor_copy(out=ident[:], in_=ident_f[:]), R=[ident_f], W=[ident])
        S.op("pool", lambda e: e.memset(ones_bf[:], 1.0), W=[ones_bf])
        S.op("dve", lambda e: e.tensor_tensor(out=gqk[:], in0=gqs[:], in1=gks[:], op=ALU.mult),
             R=[gqs, gks], W=[gqk])

        def load_w(dst_buf, dst_ap_fn, src_rows_ap, ncols, scale_buf=None, scale_col=None, key=None, eng="pool"):
            S.dma(dst_ap_fn(), src_rows_ap, W=[(dst_buf, key)], q="pool")

        ssq_pool = [sb("ssq%d" % i, [128, 4], F32) for i in range(4)]
        ssq_i = [0]

        def rmsnorm_rows(xt, xkey, xn_bf, rstd_eng="dve"):
            ssq = ssq_pool[ssq_i[0] % 4]
            ssq_i[0] += 1
            S.op("act", lambda e: e.activation(out=xn_bf[:], in_=xt[:], func=AF.Square, accum_out=ssq[:, 0:1]),
                 R=[xt], W=[xn_bf, ssq])
            S.op("act", lambda e: e.activation(out=ssq[:, 1:2], in_=ssq[:, 0:1], func=AF.Sqrt, scale=1.0 / 1024, bias=epsb[:, 0:1]),
                 R=[ssq, epsb], W=[ssq])
            S.op("dve", lambda e: e.reciprocal(out=ssq[:, 2:3], in_=ssq[:, 1:2]), R=[ssq], W=[ssq])
            S.op("act", lambda e: e.activation(out=xn_bf[:], in_=xt[:], func=AF.Copy, scale=ssq[:, 2:3]),
                 R=[xt, ssq], W=[xn_bf])

        def rms_stats(xt, xn_bf):
            ssq = ssq_pool[ssq_i[0] % 4]
            ssq_i[0] += 1
            S.op("act", lambda e: e.activation(out=xn_bf[:], in_=xt[:], func=AF.Square, accum_out=ssq[:, 0:1]),
                 R=[xt], W=[xn_bf, ssq])
            S.op("act", lambda e: e.activation(out=ssq[:, 1:2], in_=ssq[:, 0:1], func=AF.Sqrt, scale=1.0 / 1024, bias=epsb[:, 0:1]),
                 R=[ssq, epsb], W=[ssq])
            S.op("dve", lambda e: e.reciprocal(out=ssq[:, 2:3], in_=ssq[:, 1:2]), R=[ssq], W=[ssq])
            return ssq

        def rms_apply(xt, xn_bf, ssq):
            S.op("act", lambda e: e.activation(out=xn_bf[:], in_=xt[:], func=AF.Copy, scale=ssq[:, 2:3]),
                 R=[xt, ssq], W=[xn_bf])

        def transpose8(xn_bf, dst_fn, dstbuf, dkey, gbuf):
            pb = nextB()
            for dc in range(8):
                S.op("pe", lambda e, dc=dc: e.transpose(out=pb[:, dc * 128:(dc + 1) * 128],
                                                         in_=xn_bf[:, dc * 128:(dc + 1) * 128], identity=ident[:]),
                     R=[xn_bf, ident], W=[(pb, dc)])
            S.op("dve", lambda e: e.tensor_tensor(out=dst_fn(), in0=pb[:].rearrange("q (c p) -> q c p", c=8),
                                                  in1=gbuf[:].unsqueeze(2).broadcast_to([128, 8, 128]), op=ALU.mult),
                 R=[pb, gbuf], W=[(dstbuf, dkey)])

        def fm_rmsnorm(srcT, dstT, gbuf, nchunk, es2, dbuf):
            sqs = [sb("fm_sq", [128, 2048], BF16, es2), sb("fm_sq2", [128, 2048], BF16, es2)]
            rstd = sb("fm_rstd", [128, 2048], F32, es2)
            pss = [nextF() for _ in range(4)]
            for m in range(nchunk):
                sq = sqs[m % 2]
                S.op("act", lambda e, m=m, sq=sq: e.activation(out=sq[:], in_=srcT[:, m, :], func=AF.Square),
                     R=[(srcT, m)], W=[sq])
                for n in range(4):
                    S.op("pe", lambda e, m=m, n=n, sq=sq: e.matmul(pss[n][:], lhsT=ones_bf[:], rhs=sq[:, 512 * n:512 * n + 512],
                                                                    start=(m == 0), stop=(m == nchunk - 1)),
                         R=[sq, ones_bf], W=[pss[n]])
            for n in range(4):
                S.op("act", lambda e, n=n: e.activation(out=rstd[:, 512 * n:512 * n + 512], in_=pss[n][:], func=AF.Ln,
                                                        scale=1.0 / (128 * nchunk), bias=epsb[:, 0:1]),
                     R=[pss[n], epsb], W=[(rstd, n)])
                S.op("act", lambda e, n=n: e.activation(out=rstd[:, 512 * n:512 * n + 512], in_=rstd[:, 512 * n:512 * n + 512],
                                                        func=AF.Exp, scale=-0.5),
                     R=[(rstd, n)], W=[(rstd, n)])
            for m in range(nchunk):
                S.op("dve", lambda e, m=m: e.scalar_tensor_tensor(out=dstT(m), in0=srcT[:, m, :],
                                                                  scalar=gbuf[:, m:m + 1], in1=rstd[:],
                                                                  op0=ALU.mult, op1=ALU.mult),
                     R=[(srcT, m), rstd, gbuf], W=[(dbuf, m)])


        with ExitStack() as esA:
            u_bf = sb("u_bf", [128, 32, 16, 16], BF16, esA)
            UT = sb("UT", [128, 32, 2, 128], BF16, esA)
            Tm = sb("Tm", [128, 32, 2, 128], BF16, esA)
            Em = sb("Em", [128, 32, 17, 16], BF16, esA)
            dsk = sb("dsk", [128, 512], F32, esA)
            S.dma(dsk[:], p_d, W=[dsk])
            Hbf = sb("Hbf", [128, 32, 128], BF16, esA)
            wu = sb("wu", [128, 8, 512], BF16, esA)
            for dc in range(8):
                load_w(wu, lambda dc=dc: wu[:, dc, :], w_in[dc * 128:(dc + 1) * 128, 1536:2048], 512, g1s, dc, key=dc)
            V = "dve"
            with ExitStack() as e2:
                LV1 = sb("LV1", [128, 8, 2, 32], F32, e2)
                LV2 = sb("LV2", [128, 8, 2, 32], F32, e2)
                c1 = sb("c1", [128, 2, 32], F32, e2)
                c2 = sb("c2", [128, 2, 32], F32, e2)
                pc1 = sb("pc1", [128, 2, 32], F32, e2)
                pc2 = sb("pc2", [128, 2, 32], F32, e2)
                Hacc = sb("Hacc", [128, 2, 32], F32, e2)
                Fm = sb("Fm", [128, 32, 2, 128], BF16, e2)
                with ExitStack() as eb:
                    def pt(name, shape=(128, 32)):
                        return sb(name, list(shape), F32, eb)

                    def tt(o, a, b, op, eng=V):
                        S.op(eng, lambda e: e.tensor_tensor(out=o[:], in0=a[:], in1=b[:], op=op), R=[a, b], W=[o])

                    def tsc(o, a, s1, s2, op0, op1=None, eng=V):
                        if op1 is None:
                            S.op(eng, lambda e: e.tensor_scalar(out=o[:], in0=a[:], scalar1=s1, scalar2=None, op0=op0), R=[a], W=[o])
                        else:
                            S.op(eng, lambda e: e.tensor_scalar(out=o[:], in0=a[:], scalar1=s1, scalar2=s2, op0=op0, op1=op1), R=[a], W=[o])
                    lr, li, ldt = pt("lr"), pt("li"), pt("ldt")
                    BS, BW, CS, CW = pt("BS", (128, 32, 16)), pt("BW", (128, 32, 16)), pt("CS", (128, 32, 16)), pt("CW", (128, 32, 16))
                    for bfr, src in ((lr, p_lr), (li, p_li), (ldt, p_ldt), (BS, p_bs), (BW, p_bw), (CS, p_cs), (CW, p_cw)):
                        S.dma(bfr[:], src, W=[bfr])
                    dtv, mag, th, t1, t2, sn, cs_, ar, ai = [pt("pp%d" % i) for i in range(9)]
                    S.op("act", lambda e: e.activation(out=dtv[:], in_=ldt[:], func=AF.Exp), R=[ldt], W=[dtv])
                    tt(t1, lr, dtv, ALU.mult)
                    S.op("act", lambda e: e.activation(out=mag[:], in_=t1[:], func=AF.Exp), R=[t1], W=[mag])
                    tt(th, li, dtv, ALU.mult)
                    def reduce_pi(r):
                        for _ in range(5):
                            tsc(t3r, r, math.pi, 2 * math.pi, ALU.is_gt, ALU.mult)
                            tt(r, r, t3r, ALU.subtract)
                    t3r = pt("t3r")
                    tsc(t1, th, 0.0, None, ALU.add)
                    reduce_pi(t1)
                    S.op("act", lambda e: e.activation(out=sn[:], in_=t1[:], func=AF.Sin), R=[t1], W=[sn])
                    tsc(t2, th, 0.5 * math.pi, None, ALU.add)
                    reduce_pi(t2)
                    S.op("act", lambda e: e.activation(out=cs_[:], in_=t2[:], func=AF.Sin), R=[t2], W=[cs_])
                    tt(ar, mag, cs_, ALU.mult)
                    tt(ai, mag, sn, ALU.mult)
                    den, nr, cr, ci, t3 = [pt("pq%d" % i) for i in range(5)]
                    tt(den, lr, lr, ALU.mult)
                    tt(t1, li, li, ALU.mult)
                    tt(den, den, t1, ALU.add)
                    S.op(V, lambda e: e.reciprocal(out=den[:], in_=den[:]), R=[den], W=[den])
                    tsc(nr, ar, -1.0, None, ALU.add)
                    tt(t1, nr, lr, ALU.mult)
                    tt(t2, ai, li, ALU.mult)
                    tt(cr, t1, t2, ALU.add)
                    tt(cr, cr, den, ALU.mult)
                    tt(t1, ai, lr, ALU.mult)
                    tt(t2, nr, li, ALU.mult)
                    tt(ci, t1, t2, ALU.subtract)
                    tt(ci, ci, den, ALU.mult)
                    PWr, PWi = pt("PWr", (128, 32, 17)), pt("PWi", (128, 32, 17))
                    S.op(V, lambda e: e.memset(PWr[:, :, 0], 1.0), W=[PWr])
                    S.op(V, lambda e: e.memset(PWi[:, :, 0], 0.0), W=[PWi])
                    S.op(V, lambda e: e.tensor_copy(out=PWr[:, :, 1], in_=ar[:]), R=[ar], W=[PWr])
                    S.op(V, lambda e: e.tensor_copy(out=PWi[:, :, 1], in_=ai[:]), R=[ai], W=[PWi])
                    dA, dB = pt("dA", (128, 32, 8)), pt("dB", (128, 32, 8))
                    for n_ in (1, 2, 4, 8):
                        def bcr(n_=n_):
                            return PWr[:, :, n_].unsqueeze(2).broadcast_to([128, 32, n_])

                        def bci(n_=n_):
                            return PWi[:, :, n_].unsqueeze(2).broadcast_to([128, 32, n_])
                        S.op(V, lambda e, n_=n_, bcr=bcr: e.tensor_tensor(out=dA[:, :, 0:n_], in0=PWr[:, :, 1:n_ + 1], in1=bcr(), op=ALU.mult), R=[PWr], W=[dA])
                        S.op(V, lambda e, n_=n_, bci=bci: e.tensor_tensor(out=dB[:, :, 0:n_], in0=PWi[:, :, 1:n_ + 1], in1=bci(), op=ALU.mult), R=[PWi], W=[dB])
                        S.op(V, lambda e, n_=n_: e.tensor_tensor(out=PWr[:, :, n_ + 1:2 * n_ + 1], in0=dA[:, :, 0:n_], in1=dB[:, :, 0:n_], op=ALU.subtract), R=[dA, dB], W=[PWr])
                        S.op(V, lambda e, n_=n_, bci=bci: e.tensor_tensor(out=dA[:, :, 0:n_], in0=PWr[:, :, 1:n_ + 1], in1=bci(), op=ALU.mult), R=[PWr, PWi], W=[dA])
                        S.op(V, lambda e, n_=n_, bcr=bcr: e.tensor_tensor(out=dB[:, :, 0:n_], in0=PWi[:, :, 1:n_ + 1], in1=bcr(), op=ALU.mult), R=[PWi, PWr], W=[dB])
                        S.op(V, lambda e, n_=n_: e.tensor_tensor(out=PWi[:, :, n_ + 1:2 * n_ + 1], in0=dA[:, :, 0:n_], in1=dB[:, :, 0:n_], op=ALU.add), R=[dA, dB], W=[PWi])
                    Nr, Ni, m2 = pt("Nr", (128, 32, 16)), pt("Ni", (128, 32, 16)), pt("m2", (128, 32, 16))
                    m3 = pt("m3", (128, 32, 16))
                    S.op(V, lambda e: e.tensor_tensor(out=m2[:], in0=PWr[:, :, 0:16], in1=PWr[:, :, 0:16], op=ALU.mult), R=[PWr], W=[m2])
                    S.op(V, lambda e: e.tensor_tensor(out=m3[:], in0=PWi[:, :, 0:16], in1=PWi[:, :, 0:16], op=ALU.mult), R=[PWi], W=[m3])
                    tt(m2, m2, m3, ALU.add)
                    S.op(V, lambda e: e.reciprocal(out=m2[:], in_=m2[:]), R=[m2], W=[m2])
                    S.op(V, lambda e: e.tensor_tensor(out=Nr[:], in0=PWr[:, :, 0:16], in1=m2[:], op=ALU.mult), R=[PWr, m2], W=[Nr])
                    S.op(V, lambda e: e.tensor_tensor(out=Ni[:], in0=PWi[:, :, 0:16], in1=m2[:], op=ALU.mult), R=[PWi, m2], W=[Ni])
                    tsc(Ni, Ni, -1.0, None, ALU.mult)
                    a16r, a16is, a15r, a15is, ars, ais = [pt("pa%d" % i) for i in range(6)]
                    S.op(V, lambda e: e.tensor_copy(out=a16r[:], in_=PWr[:, :, 16]), R=[PWr], W=[a16r])
                    S.op(V, lambda e: e.tensor_scalar(out=a16is[:], in0=PWi[:, :, 16], scalar1=sgn[:, 0:1], scalar2=None, op0=ALU.mult), R=[PWi, sgn], W=[a16is])
                    S.op(V, lambda e: e.tensor_copy(out=a15r[:], in_=PWr[:, :, 15]), R=[PWr], W=[a15r])
                    S.op(V, lambda e: e.tensor_scalar(out=a15is[:], in0=PWi[:, :, 15], scalar1=sgn[:, 0:1], scalar2=None, op0=ALU.mult), R=[PWi, sgn], W=[a15is])
                    BbS, sBW, tb1, tb2 = pt("BbS", (128, 32, 16)), pt("sBW", (128, 32, 16)), pt("tb1", (128, 32, 16)), pt("tb2", (128, 32, 16))
                    sCS = pt("sCS", (128, 32, 16))
                    S.op(V, lambda e: e.tensor_scalar(out=sBW[:], in0=BW[:], scalar1=sgn[:, 0:1], scalar2=None, op0=ALU.mult), R=[BW, sgn], W=[sBW])
                    S.op(V, lambda e: e.tensor_scalar(out=sCS[:], in0=CS[:], scalar1=sgn[:, 1:2], scalar2=None, op0=ALU.mult), R=[CS, sgn], W=[sCS])

                    def bc16(b):
                        return b[:].unsqueeze(2).broadcast_to([128, 32, 16])
                    S.op(V, lambda e: e.tensor_tensor(out=tb1[:], in0=BS[:], in1=bc16(cr), op=ALU.mult), R=[BS, cr], W=[tb1])
                    S.op(V, lambda e: e.tensor_tensor(out=tb2[:], in0=sBW[:], in1=bc16(ci), op=ALU.mult), R=[sBW, ci], W=[tb2])
                    tt(BbS, tb1, tb2, ALU.add)

                    BbW = sb("BbW", [128, 32, 16], F32, eb)
                    S.dma(BbW[0:64], BbS[64:128], R=[BbS], W=[(BbW, 0)])
                    S.dma(BbW[64:128], BbS[0:64], R=[BbS], W=[(BbW, 1)])
                    sBbW = sb("sBbW", [128, 32, 16], F32, eb)
                    S.op(V, lambda e: e.tensor_scalar(out=sBbW[:], in0=BbW[:], scalar1=sgn[:, 0:1], scalar2=None, op0=ALU.mult), R=[BbW, sgn], W=[sBbW])
                    tmk = sb("tmask", [128, 128], F32, eb)
                    S.dma(tmk[:], c_tmask, W=[tmk])
                    btemps = []
                    for bi in range(2):
                        btemps.append((sb("Wa", [128, 4, 17, 16], F32, eb), sb("Wb", [128, 4, 17, 16], F32, eb),
                                       sb("Ya", [128, 4, 16, 16], F32, eb), sb("Yb", [128, 4, 16, 16], F32, eb),
                                       sb("Yw", [128, 4, 16, 16], F32, eb), sb("Ybf", [128, 4, 16, 16], BF16, eb),
                                       sb("Wbf", [128, 4, 16, 16], BF16, eb), sb("FTb", [128, 4, 16, 16], BF16, eb)))

                    def batch(qd):
                        gs = slice(4 * qd, 4 * qd + 4)
                        if True:
                            Wa, Wb, Ya, Yb, Yw, Ybf, Wbf, FTb = btemps[qd % 2]

                            def bk(b, n):
                                return b[:, gs, 0:n].unsqueeze(3).broadcast_to([128, 4, n, 16])

                            def bcm(b, n):
                                return b[:, gs, :].unsqueeze(2).broadcast_to([128, 4, n, 16])
                            S.op(V, lambda e: e.tensor_tensor(out=Wa[:], in0=bk(PWr, 17), in1=bcm(sCS, 17), op=ALU.mult), R=[PWr, sCS], W=[Wa])
                            S.op("pool", lambda e: e.tensor_tensor(out=Wb[:], in0=bk(PWi, 17), in1=bcm(CW, 17), op=ALU.mult), R=[PWi, CW], W=[Wb])
                            S.op(V, lambda e: e.tensor_tensor(out=Em[:, gs], in0=Wa[:], in1=Wb[:], op=ALU.subtract), R=[Wa, Wb], W=[(Em, qd)])
                            S.op(V, lambda e: e.tensor_tensor(out=Ya[:], in0=bk(Nr, 16), in1=bcm(BbS, 16), op=ALU.mult), R=[Nr, BbS], W=[Ya])
                            S.op("pool", lambda e: e.tensor_tensor(out=Yb[:], in0=bk(Ni, 16), in1=bcm(sBbW, 16), op=ALU.mult), R=[Ni, sBbW], W=[Yb])
                            S.op(V, lambda e: e.tensor_tensor(out=Ya[:], in0=Ya[:], in1=Yb[:], op=ALU.add), R=[Ya, Yb], W=[Ya])
                            S.op("act", lambda e: e.activation(out=Ybf[:], in_=Ya[:], func=AF.Copy), R=[Ya], W=[Ybf])
                            S.op("act", lambda e: e.activation(out=Wbf[:], in_=Em[:, gs, 0:16, :], func=AF.Copy), R=[(Em, qd)], W=[Wbf])
                            S.dma(Yw[0:64], Ya[64:128], R=[Ya], W=[(Yw, 0)])
                            S.dma(Yw[64:128], Ya[0:64], R=[Ya], W=[(Yw, 1)])
                            def b15(b):
                                return b[:, gs].unsqueeze(2).unsqueeze(3).broadcast_to([128, 4, 16, 16])
                            S.op(V, lambda e: e.tensor_tensor(out=Yb[:], in0=Ya[:], in1=b15(a15r), op=ALU.mult), R=[Ya, a15r], W=[Yb])
                            S.op("pool", lambda e: e.tensor_tensor(out=Yw[:], in0=Yw[:], in1=b15(a15is), op=ALU.mult), R=[Yw, a15is], W=[Yw])
                            S.op(V, lambda e: e.tensor_tensor(out=FTb[:], in0=Yb[:], in1=Yw[:], op=ALU.add), R=[Yb, Yw], W=[FTb])
                            for gl in range(4):
                                g = 4 * qd + gl
                                pb = nextB()
                                for J in range(2):
                                    S.op("pe", lambda e, gl=gl, J=J, pb=pb: e.transpose(
                                        out=pb[:, 128 * J:128 * J + 128],
                                        in_=FTb[:, gl, 8 * J:8 * J + 8, :].rearrange("q j c -> q (j c)"), identity=ident[:]),
                                         R=[FTb, ident], W=[(pb, J)])
                                S.op("dve", lambda e, g=g, pb=pb: e.tensor_copy(
                                    out=Fm[:, g, :, :].rearrange("q J k -> q (J k)"), in_=pb[:, 0:256]),
                                     R=[pb], W=[(Fm, g)])
                                pf = nextF()
                                for tb in range(2):
                                    S.op("pe", lambda e, gl=gl, tb=tb, pf=pf: e.matmul(
                                        pf[:, 128 * tb:128 * tb + 128],
                                        lhsT=Ybf[:, gl, 0:8, :].rearrange("q j c -> q (j c)"),
                                        rhs=Wbf[:, gl, 8 * tb:8 * tb + 8, :].rearrange("q j c -> q (j c)"),
                                        start=True, stop=True), R=[Ybf, Wbf], W=[(pf, tb)])
                                S.op("dve", lambda e, g=g, pf=pf: e.tensor_tensor(out=Tm[:, g, 0, :], in0=pf[:, 0:128], in1=tmk[:], op=ALU.mult),
                                     R=[pf, tmk], W=[(Tm, g)])
                                S.op("act", lambda e, g=g, pf=pf: e.activation(out=Tm[:, g, 1, :], in_=pf[:, 128:256], func=AF.Copy),
                                     R=[pf], W=[(Tm, g)])
                    for qd in range(8):
                        batch(qd)
                    qr, qi, q1, q2 = pt("qr"), pt("qi"), pt("q1"), pt("q2")
                    S.op(V, lambda e: e.tensor_copy(out=qr[:], in_=PWr[:, :, 16]), R=[PWr], W=[qr])
                    S.op(V, lambda e: e.tensor_copy(out=qi[:], in_=PWi[:, :, 16]), R=[PWi], W=[qi])
                    for L in range(8):
                        S.op(V, lambda e, L=L: e.tensor_copy(out=LV1[:, L, 0, :], in_=qr[:]), R=[qr], W=[LV1])
                        S.op(V, lambda e, L=L: e.tensor_copy(out=LV1[:, L, 1, :], in_=qr[:]), R=[qr], W=[LV1])
                        S.op(V, lambda e, L=L: e.tensor_scalar(out=LV2[:, L, 0, :], in0=qi[:], scalar1=sgn[:, 0:1], scalar2=None, op0=ALU.mult), R=[qi, sgn], W=[LV2])
                        S.op(V, lambda e, L=L: e.tensor_scalar(out=LV2[:, L, 1, :], in0=qi[:], scalar1=sgn[:, 1:2], scalar2=None, op0=ALU.mult), R=[qi, sgn], W=[LV2])
                        if L < 7:
                            tt(q1, qr, qr, ALU.mult)
                            tt(q2, qi, qi, ALU.mult)
                            tt(qi, qr, qi, ALU.mult)
                            tsc(qi, qi, 2.0, None, ALU.mult)
                            tt(qr, q1, q2, ALU.subtract)

                def cmul_acc(dst, src, L, add, n=None, eng=None):
                    if n is None:
                        k1, k2a, k2b = LV1[:, L], LV2[:, L, 0, :], LV2[:, L, 1, :]
                        cc1, cc2 = (pc1, pc2) if eng == 'pool' else (c1, c2)
                        t1_, t2_ = cc1[:], cc2[:]
                        t2a, t2b = cc2[:, 0, :], cc2[:, 1, :]
                        sa, sb_ = (lambda: src()[:, 1, :]), (lambda: src()[:, 0, :])
                    else:
                        k1 = LV1[:, L].unsqueeze(1).broadcast_to([128, n, 2, 32])
                        k2a = LV2[:, L, 0, :].unsqueeze(1).broadcast_to([128, n, 32])
                        k2b = LV2[:, L, 1, :].unsqueeze(1).broadcast_to([128, n, 32])
                        t1_, t2_ = tc1[:, 0:n], tc2[:, 0:n]
                        t2a, t2b = tc2[:, 0:n, 0, :], tc2[:, 0:n, 1, :]
                        sa, sb_ = (lambda: src()[:, :, 1, :]), (lambda: src()[:, :, 0, :])
                    tb1_, tb2_ = (((pc1, pc2) if eng == 'pool' else (c1, c2)) if n is None else (tc1, tc2))
                    S.op(eng or V, lambda e: e.tensor_tensor(out=t1_, in0=src(), in1=k1, op=ALU.mult), R=[srcbuf[0], LV1], W=[tb1_])
                    S.op(eng or V, lambda e: e.tensor_tensor(out=t2a, in0=sa(), in1=k2a, op=ALU.mult), R=[srcbuf[0], LV2], W=[tb2_])
                    S.op(eng or V, lambda e: e.tensor_tensor(out=t2b, in0=sb_(), in1=k2b, op=ALU.mult), R=[srcbuf[0], LV2], W=[tb2_])
                    if add is not None:
                        S.op(eng or V, lambda e: e.tensor_tensor(out=t1_, in0=t1_, in1=add(), op=ALU.add), R=[tb1_, srcbuf[1]], W=[tb1_])
                    S.op(eng or V, lambda e: e.tensor_tensor(out=dst(), in0=t1_, in1=t2_, op=ALU.add), R=[tb1_, tb2_], W=[srcbuf[2]])

                srcbuf = [None, None, None]
                SS = sb("SS", [128, 2, 128, 32], F32, e2)
                with ExitStack() as ep:
                    TT0 = sb("TT0", [128, 64, 2, 32], F32, ep)
                    tc1 = sb("tc1", [128, 32, 2, 32], F32, ep)
                    tc2 = sb("tc2", [128, 32, 2, 32], F32, ep)
                    TT1 = sb("TT1", [128, 32, 2, 32], F32, ep)
                    xts = [sb("xtA%d" % i, [128, 1024], F32, ep) for i in range(NB_)]
                    xnb = [sb("xnbA%d" % i, [128, 1024], BF16, ep) for i in range(NB_)]
                    xTt = [sb("xTtA%d" % i, [128, 8, 128], BF16, ep) for i in range(2)]

                    def uproj_chunk(src_t):
                        sq_of = {}
                        pus = {}
                        for j in range(16 + 4):
                            if j < 16:
                                xt, xn = xts[j % NB_], xnb[j % NB_]
                                S.dma(xt[:], src_t[:, j, :], W=[xt])
                                sq_of[j] = rms_stats(xt, xn)
                            j1 = j - 1
                            if 0 <= j1 < 16:
                                xt, xn, xT = xts[j1 % NB_], xnb[j1 % NB_], xTt[j1 % 2]
                                rms_apply(xt, xn, sq_of[j1])
                                transpose8(xn, lambda xT=xT: xT[:], xT, None, g1s)
                            jj = j - 2
                            if 0 <= jj < 16:
                                xT2 = xTt[jj % 2]
                                pu = nextF()
                                pus[jj] = pu
                                for dc in range(8):
                                    S.op("pe", lambda e, dc=dc, xT2=xT2, pu=pu: e.matmul(pu[:], lhsT=xT2[:, dc, :], rhs=wu[:, dc, :],
                                                                                     start=(dc == 0), stop=(dc == 7)),
                                         R=[xT2, (wu, dc)], W=[pu])
                            j3 = j - 3
                            if 0 <= j3 < 16:
                                pu3 = pus[j3]
                                S.op("act", lambda e, j3=j3, pu3=pu3: e.activation(
                                    out=u_bf[:, :, j3, :], in_=pu3[:].rearrange("q (g c) -> q g c", g=32), func=AF.Copy),
                                     R=[pu3], W=[(u_bf, j3)])
                        for g in range(32):
                            if g % 4 == 0:
                                pb = nextB()
                            for J in range(2):
                                o = ((g % 4) * 2 + J) * 128
                                S.op("pe", lambda e, g=g, J=J, o=o, pb=pb: e.transpose(
                                    out=pb[:, o:o + 128],
                                    in_=u_bf[:, g, 8 * J:8 * J + 8, :].rearrange("q j c -> q (j c)"), identity=ident[:]),
                                     R=[u_bf, ident], W=[(pb, (g % 4) * 2 + J)])
                            if g % 4 == 3:
                                g0 = g - 3
                                S.op("dve", lambda e, g0=g0, pb=pb: e.tensor_copy(
                                    out=UT[:, g0:g0 + 4, :, :].rearrange("q g J k -> q (g J k)"), in_=pb[:]),
                                     R=[pb], W=[(UT, g0 // 4)])

                    def s1_chunk():
                        for g in range(32):
                            if g % 4 == 0:
                                pf = nextF()
                            o = (g % 4) * 128
                            for J in range(2):
                                S.op("pe", lambda e, g=g, J=J, o=o, pf=pf: e.matmul(pf[:, o:o + 128], lhsT=Fm[:, g, J, :], rhs=UT[:, g, J, :],
                                                                                 start=(J == 0), stop=(J == 1)),
                                     R=[(Fm, g), (UT, g // 4)], W=[(pf, g % 4)])
                            if g % 4 == 3:
                                g0 = g - 3
                                S.op("dve", lambda e, g0=g0, pf=pf: e.tensor_copy(
                                    out=SS[:, 0, :, g0:g0 + 4].rearrange("q k g -> q g k"),
                                    in_=pf[:].rearrange("q (g k) -> q g k", g=4)), R=[pf], W=[(SS, ("s", g0))])
                        S.dma(SS[0:64, 1, :, :], SS[64:128, 0, :, :], R=[SS], W=[(SS, "w0")])
                        S.dma(SS[64:128, 1, :, :], SS[0:64, 0, :, :], R=[SS], W=[(SS, "w1")])

                    def kview(buf, n):
                        if buf is SS:
                            return SS[:, :, 0:n, :].rearrange("q s k g -> q k s g")
                        return buf[:, 0:n]

                    def tree():
                        bufs = [TT0, TT1]
                        n = 128
                        for L in range(7):
                            src_b = SS if L == 0 else bufs[(L + 1) % 2]
                            dst_b = bufs[L % 2]
                            no = n // 2
                            for h0 in range(0, no, 32):
                                hn = min(32, no - h0)
                                ev = lambda src_b=src_b, n=n, h0=h0, hn=hn: kview(src_b, n).rearrange("q (m two) s g -> q m two s g", two=2)[:, h0:h0 + hn, 0]
                                od = lambda src_b=src_b, n=n, h0=h0, hn=hn: kview(src_b, n).rearrange("q (m two) s g -> q m two s g", two=2)[:, h0:h0 + hn, 1]
                                ds = lambda dst_b=dst_b, h0=h0, hn=hn: dst_b[:, h0:h0 + hn]
                                srcbuf[0], srcbuf[1], srcbuf[2] = src_b, src_b, dst_b
                                cmul_acc(ds, ev, L, od, n=hn, eng=TREE_ENG)
                            n = no
                        return bufs[6 % 2]

                    S.op(V, lambda e: e.memset(Hacc[:], 0.0), W=[Hacc])
                    for i in range(7):
                        uproj_chunk(xp_t[i])
                        s1_chunk()
                        rb = tree()
                        srcbuf[0], srcbuf[1], srcbuf[2] = Hacc, rb, Hacc
                        cmul_acc(lambda: Hacc[:], lambda: Hacc[:], 7, lambda rb=rb: rb[:, 0], eng=TREE_ENG)
                    uproj_chunk(xm_t)
                    s1_chunk()
                XX = sb("XX", [128, 129, 2, 32], F32, e2)
                S.op(V, lambda e: e.tensor_copy(out=XX[:, 0], in_=Hacc[:]), R=[Hacc], W=[XX])
                for k in range(128):
                    srcbuf[0], srcbuf[1], srcbuf[2] = XX, SS, XX
                    cmul_acc(lambda k=k: XX[:, k + 1], lambda k=k: XX[:, k], 0, lambda k=k: SS[:, :, k, :])
                S.op(V, lambda e: e.tensor_copy(out=Hbf[:], in_=XX[:, 0:128, 0, :].rearrange("q k g -> q g k")), R=[XX], W=[Hbf])
                if DEBUG:
                    S.dma(dbg["hfin"][:, 0:32], XX[:, 128, 0, :], R=[XX])
                    S.dma(dbg["hfin"][:, 32:64], Hacc[:, 0, :], R=[Hacc])

            ztm = sb("ztm", [128, 16, 512], BF16, esA)
            with ExitStack() as e4:
                yb = [sb("yb%d" % i, [128, 2, 256], F32, e4) for i in range(3)]
                g2b = [sb("g2b%d" % i, [128, 2, 256], F32, e4) for i in range(3)]
                for gp in range(16):
                    pf = nextF()
                    y, t = yb[gp % 3], g2b[gp % 3]
                    for gl in range(2):
                        g = 2 * gp + gl
                        o = 256 * gl
                        S.op("pe", lambda e, g=g, o=o, pf=pf: e.matmul(pf[:, o:o + 128], lhsT=UT[:, g, 0, :], rhs=Tm[:, g, 0, :], start=True, stop=False),
                             R=[(UT, g // 4), (Tm, g)], W=[(pf, gl)])
                        S.op("pe", lambda e, g=g, o=o, pf=pf: e.matmul(pf[:, o:o + 128], lhsT=Hbf[:, g, :],
                                                                       rhs=Em[:, g, 1:9, :].rearrange("q t c -> q (t c)"), start=False, stop=True),
                             R=[Hbf, (Em, g // 4)], W=[(pf, gl)])
                        S.op("pe", lambda e, g=g, o=o, pf=pf: e.matmul(pf[:, o + 128:o + 256], lhsT=UT[:, g, 0, :], rhs=Tm[:, g, 1, :], start=True, stop=False),
                             R=[(UT, g // 4), (Tm, g)], W=[(pf, gl)])
                        S.op("pe", lambda e, g=g, o=o, pf=pf: e.matmul(pf[:, o + 128:o + 256], lhsT=UT[:, g, 1, :], rhs=Tm[:, g, 0, :], start=False, stop=False),
                             R=[(UT, g // 4), (Tm, g)], W=[(pf, gl)])
                        S.op("pe", lambda e, g=g, o=o, pf=pf: e.matmul(pf[:, o + 128:o + 256], lhsT=Hbf[:, g, :],
                                                                       rhs=Em[:, g, 9:17, :].rearrange("q t c -> q (t c)"), start=False, stop=True),
                             R=[Hbf, (Em, g // 4)], W=[(pf, gl)])
                    ug = ztm[:, :, 32 * gp:32 * gp + 32].rearrange("q j (g c) -> q g j c", g=2)
                    dg = dsk[:, 32 * gp:32 * gp + 32].rearrange("q (g c) -> q g c", g=2).unsqueeze(2).broadcast_to([128, 2, 16, 16])
                    S.op("dve", lambda e, gp=gp, y=y, dg=dg: e.tensor_tensor(
                        out=y[:].rearrange("q g (j c) -> q g j c", j=16), in0=u_bf[:, 2 * gp:2 * gp + 2], in1=dg, op=ALU.mult),
                         R=[u_bf, dsk], W=[y])
                    S.op("dve", lambda e, y=y, pf=pf: e.tensor_tensor(out=y[:], in0=y[:], in1=pf[:].rearrange("q (g x) -> q g x", g=2), op=ALU.add),
                         R=[y, pf], W=[y])
                    S.op("pool", lambda e, y=y, t=t: e.tensor_tensor(out=t[:], in0=y[:], in1=y[:], op=ALU.mult), R=[y], W=[t])
                    S.op("pool", lambda e, t=t: e.tensor_scalar(out=t[:], in0=t[:], scalar1=0.044715, scalar2=1.0, op0=ALU.mult, op1=ALU.add), R=[t], W=[t])
                    S.op("pool", lambda e, y=y, t=t: e.tensor_tensor(out=t[:], in0=t[:], in1=y[:], op=ALU.mult), R=[t, y], W=[t])
                    S.op("act", lambda e, t=t: e.activation(out=t[:], in_=t[:], func=AF.Sigmoid, scale=1.5957691216), R=[t], W=[t])
                    S.op("dve", lambda e, y=y, t=t, ug=ug: e.tensor_tensor(
                        out=ug, in0=y[:].rearrange("q g (j c) -> q g j c", j=16), in1=t[:].rearrange("q g (j c) -> q g j c", j=16), op=ALU.mult),
                         R=[y, t], W=[(ztm, gp)])
                if DEBUG:
                    dbg_dump("u", ztm, lambda m: ztm[:, 4 * m:4 * m + 4, :].rearrange("q j x -> q (j x)"), e4)
            zT = sb("zT", [128, 4, 2048], BF16, esA)
            for j in range(16):
                pb = nextB()
                for m in range(4):
                    S.op("pe", lambda e, j=j, m=m, pb=pb: e.transpose(
                        out=pb[:, 128 * m:128 * m + 128],
                        in_=ztm[:, j, 128 * m:128 * m + 128], identity=ident[:]),
                         R=[ztm, ident], W=[(pb, m)])
                S.op("dve", lambda e, j=j, pb=pb: e.tensor_copy(out=zT[:, :, 128 * j:128 * j + 128],
                                                                 in_=pb[:, 0:512].rearrange("q (m p) -> q m p", m=4)),
                     R=[pb], W=[(zT, j // 4)])
            with ExitStack() as e5:
                wg = sb("wg", [128, 4, 512], BF16, e5)
                for kc in range(4):
                    load_w(wg, lambda kc=kc: wg[:, kc, :], glu_w[kc * 128:(kc + 1) * 128, :], 512, key=kc)
                sT = sb("sT", [128, 4, 2048], F32, e5)
                sg = [sb("sg%d" % i, [128, 512], F32, e5) for i in range(4)]
                for n in range(4):
                    for m in range(4):
                        pf = nextF()
                        for kc in range(4):
                            S.op("pe", lambda e, n=n, m=m, kc=kc, pf=pf: e.matmul(
                                pf[:], lhsT=wg[:, kc, 128 * m:128 * m + 128], rhs=zT[:, kc, 512 * n:512 * n + 512],
                                start=(kc == 0), stop=(kc == 3)), R=[(wg, kc), (zT, n)], W=[pf])
                        s_ = sg[(n * 4 + m) % 4]
                        S.op("act", lambda e, m=m, pf=pf, s_=s_: e.activation(out=s_[:], in_=pf[:], func=AF.Sigmoid,
                                                                             bias=glubs[:, m:m + 1]), R=[pf, glubs], W=[s_])
                        S.op("dve", lambda e, n=n, m=m, s_=s_: e.tensor_tensor(out=sT[:, m, 512 * n:512 * n + 512],
                                                                              in0=zT[:, m, 512 * n:512 * n + 512], in1=s_[:], op=ALU.mult),
                             R=[s_, (zT, n)], W=[(sT, m)])
                ssmN = sb("ssmN", [128, 4, 2048], BF16, e5)
                fm_rmsnorm(sT, lambda m: ssmN[:, m, :], gsos, 4, e5, ssmN)
                S.dma(ssmd, ssmN[:].rearrange("q m f -> q (m f)"), R=[ssmN], W=[ssmdb])
                if DEBUG:
                    dbg_dump("ssm", ssmN, lambda m: ssmN[:, m, :], e5)

        with ExitStack() as esB:
            xnT = sb("xnT", [128, 8, 4096], BF16, esB)
            attT = sb("attT", [128, 4, 2048], F32, esB)
            attN = sb("attN", [128, 4, 2048], BF16, esB)
            VA = sb("VA", [128, 69, 128], BF16, esB)
            VB = sb("VB", [128, 69, 128], BF16, esB)
            S.op("pool", lambda e: e.memset(VA[:, :, 64:128], 0.0), W=[VA])
            S.op("pool", lambda e: e.memset(VB[:, :, 0:64], 0.0), W=[VB])
            maskA = sb("maskA", [128, 256], BF16, esB)
            maskH = sb("maskH", [128, 256], BF16, esB)
            selA = sb("selA", [128, 128], BF16, esB)
            selB = sb("selB", [128, 128], BF16, esB)
            with ExitStack() as e1:
                mt = sb("mt", [128, 256], F32, e1)
                for dstb, src, n in ((maskA, c_maskA, 256), (maskH, c_maskH, 256), (selA, c_selA, 128), (selB, c_selB, 128)):
                    S.dma(mt[:, :n], src, W=[mt])
                    S.op("dve", lambda e, dstb=dstb, n=n: e.tensor_copy(out=dstb[:], in_=mt[:, :n]), R=[mt], W=[dstb])
                xts = [sb("xtB%d" % i, [128, 1024], F32, e1) for i in range(3)]
                xnb = [sb("xnbB%d" % i, [128, 1024], BF16, e1) for i in range(3)]
                tl = [(hm, src, j) for hm, src in ((0, xh_t), (1, xm_t)) for j in range(16)]
                sq_of = {}
                for n in range(len(tl) + 1):
                    if n < len(tl):
                        hm, src, j = tl[n]
                        S.dma(xts[n % 3][:], src[:, j, :], W=[xts[n % 3]])
                        sq_of[n] = rms_stats(xts[n % 3], xnb[n % 3])
                    if n >= 1:
                        hm, src, j = tl[n - 1]
                        xt, xn = xts[(n - 1) % 3], xnb[(n - 1) % 3]
                        rms_apply(xt, xn, sq_of[n - 1])
                        transpose8(xn, lambda hm=hm, j=j: xnT[:, :, 2048 * hm:2048 * hm + 2048].rearrange(
                            "q c (p j) -> q c p j", j=16)[:, :, :, j], xnT, (hm, j // 4), g1s)

            with ExitStack() as e2:
                wq = sb("wq", [128, 8, 128], BF16, e2)
                wk = sb("wk", [128, 8, 128], BF16, e2)
                wv = sb("wv", [128, 8, 128], BF16, e2)
                qT = sb("qT", [128, 2048], BF16, e2)
                kT = sb("kT", [128, 4096], BF16, e2)
                accN = sb("accN", [128, 2048], F32, e2)
                accD = sb("accD", [128, 2048], F32, e2)
                sqb = sb("sqb", [128, 512], BF16, e2)
                rs = sb("rs", [128, 512], F32, e2)
                PT = [sb("PT%d" % i, [128, 512], BF16, e2) for i in range(3)]
                blockones = sb("blockones", [128, 128], BF16, e2)
                S.op("pool", lambda e: e.memset(blockones[:], 0.0), W=[blockones])
                S.op("pool", lambda e: e.memset(blockones[0:64, 0:64], 1.0), W=[blockones])
                S.op("pool", lambda e: e.memset(blockones[64:128, 64:128], 1.0), W=[blockones])

                vblocks = []
                vidx = {}
                for d in (1, 4, 16):
                    nres = 1 if d == 1 else d
                    nb = 16 // nres if d != 16 else 1
                    nb = {1: 16, 4: 4, 16: 1}[d]
                    for r in range(nres):
                        vidx[(d, r, -1)] = len(vblocks)
                        vblocks.append((0, d, r, nb - 1))
                        for b in range(nb):
                            vidx[(d, r, b)] = len(vblocks)
                            vblocks.append((2048, d, r, b))
                assert len(vblocks) == 69

                for hp in range(4):
                    for dc in range(8):
                        rows = w_in[dc * 128:(dc + 1) * 128, :]
                        load_w(wq, lambda dc=dc: wq[:, dc, :], rows[:, 128 * hp:128 * hp + 128], 128, g1s, dc, key=dc)
                        load_w(wk, lambda dc=dc: wk[:, dc, :], rows[:, 512 + 128 * hp:512 + 128 * hp + 128], 128, g1s, dc, key=dc)
                        load_w(wv, lambda dc=dc: wv[:, dc, :], rows[:, 1024 + 128 * hp:1024 + 128 * hp + 128], 128, g1s, dc, key=dc)

                    def qk_proj(wb, dst, f_src0, f_dst0, gain):
                        for n in range(4):
                            pf = nextF()
                            for dc in range(8):
                                S.op("pe", lambda e, dc=dc, pf=pf, n=n: e.matmul(
                                    pf[:], lhsT=wb[:, dc, :], rhs=xnT[:, dc, f_src0 + 512 * n:f_src0 + 512 * n + 512],
                                    start=(dc == 0), stop=(dc == 7)), R=[(wb, dc), xnT], W=[pf])
                            S.op("act", lambda e, pf=pf: e.activation(out=sqb[:], in_=pf[:], func=AF.Square), R=[pf], W=[sqb])
                            p2 = nextF()
                            S.op("pe", lambda e, p2=p2: e.matmul(p2[:], lhsT=blockones[:], rhs=sqb[:], start=True, stop=True),
                                 R=[blockones, sqb], W=[p2])
                            S.op("act", lambda e, p2=p2: e.activation(out=rs[:], in_=p2[:], func=AF.Ln, scale=1.0 / 64, bias=epsb[:, 0:1]),
                                 R=[p2, epsb], W=[rs])
                            S.op("act", lambda e: e.activation(out=rs[:], in_=rs[:], func=AF.Exp, scale=-0.5), R=[rs], W=[rs])
                            o = f_dst0 + 512 * n
                            if gain is None:
                                S.op("dve", lambda e, pf=pf, o=o: e.tensor_tensor(out=dst[:, o:o + 512], in0=pf[:], in1=rs[:], op=ALU.mult),
                                     R=[pf, rs], W=[(dst, o // 512)])
                            else:
                                S.op("dve", lambda e, pf=pf, o=o: e.scalar_tensor_tensor(
                                    out=dst[:, o:o + 512], in0=pf[:], scalar=gain[:, 0:1], in1=rs[:], op0=ALU.mult, op1=ALU.mult),
                                     R=[pf, rs, gain], W=[(dst, o // 512)])
                    qk_proj(wq, qT, 2048, 0, gqk)
                    qk_proj(wk, kT, 0, 0, None)
                    qk_proj(wk, kT, 2048, 2048, None)
                    for vi, (fb, d, r, b) in enumerate(vblocks):
                        pf = nextF()
                        for dc in range(8):
                            S.op("pe", lambda e, dc=dc, pf=pf, fb=fb, d=d, r=r, b=b: e.matmul(
                                pf[:, 0:128], lhsT=blk_ap(xnT[:, dc, fb:fb + 2048], d, r, b), rhs=wv[:, dc, :],
                                start=(dc == 0), stop=(dc == 7)), R=[xnT, (wv, dc)], W=[pf])
                        S.op("act", lambda e, vi=vi, pf=pf: e.activation(out=VA[:, vi, 0:64], in_=pf[:, 0:64], func=AF.Copy),
                             R=[pf], W=[(VA, vi)])
                        S.op("dve", lambda e, vi=vi, pf=pf: e.tensor_copy(out=VB[:, vi, 64:128], in_=pf[:, 64:128]),
                             R=[pf], W=[(VB, vi)])
                    qlist = [(d, g, qix, r, b) for d in (1, 4, 16) for g in range(4) for qix, (r, b) in enumerate(qblocks(d, g))]

                    def stage_S(n):
                        d, g, qix, r, b = qlist[n]
                        halo = (b == 0)
                        ps = psF[4 + n % 2]
                        pt_ = PT[n % 3]
                        mk = maskH if halo else maskA
                        for h in range(2):
                            rs_ = slice(64 * h, 64 * h + 64)
                            o = 256 * h
                            S.op("pe", lambda e, o=o, ps=ps, mk=mk: e.matmul(ps[:, o:o + 256], lhsT=ident[:], rhs=mk[:], start=True, stop=False),
                                 R=[ident, mk], W=[(ps, h)])
                            if halo:
                                nbk = {1: 16, 4: 4, 16: 1}[d]
                                kprev = blk_ap(kT[rs_, 0:2048], d, r, nbk - 1)
                            else:
                                kprev = blk_ap(kT[rs_, 2048:4096], d, r, b - 1)
                            kcur = blk_ap(kT[rs_, 2048:4096], d, r, b)
                            qa = blk_ap(qT[rs_, :], d, r, b)
                            S.op("pe", lambda e, o=o, ps=ps, kprev=kprev, qa=qa: e.matmul(ps[:, o:o + 128], lhsT=kprev, rhs=qa, start=False, stop=False),
                                 R=[kT, qT], W=[(ps, h)])
                            S.op("pe", lambda e, o=o, ps=ps, kcur=kcur, qa=qa: e.matmul(ps[:, o + 128:o + 256], lhsT=kcur, rhs=qa, start=False, stop=True),
                                 R=[kT, qT], W=[(ps, h)])
                        S.op("act", lambda e, ps=ps, pt_=pt_: e.activation(out=pt_[:], in_=ps[:], func=AF.Exp, scale=0.125),
                             R=[ps], W=[pt_])

                    def stage_PV(n):
                        d, g, qix, r, b = qlist[n]
                        grp = n // 4
                        pn = psF[grp % 2]
                        pd = psF[2 + grp % 2]
                        pt_ = PT[n % 3]
                        vp = vidx[(d, r, b - 1)]
                        vc = vidx[(d, r, b)]
                        oq = 128 * qix
                        seq = [(VA, vp, 0), (VA, vc, 128), (VB, vp, 256), (VB, vc, 384)]
                        for si, (Vb, vi, po) in enumerate(seq):
                            S.op("pe", lambda e, Vb=Vb, vi=vi, po=po, oq=oq, pn=pn, pt_=pt_, si=si: e.matmul(
                                pn[:, oq:oq + 128], lhsT=Vb[:, vi, :], rhs=pt_[:, po:po + 128], start=(si == 0), stop=(si == 3)),
                                 R=[(Vb, vi), pt_], W=[(pn, qix)])
                        for si, (sel, po) in enumerate([(selA, 0), (selA, 128), (selB, 256), (selB, 384)]):
                            S.op("pe", lambda e, sel=sel, po=po, oq=oq, pd=pd, pt_=pt_, si=si: e.matmul(
                                pd[:, oq:oq + 128], lhsT=sel[:], rhs=pt_[:, po:po + 128], start=(si == 0), stop=(si == 3)),
                                 R=[sel, pt_], W=[(pd, qix)])
                        if qix == 3:
                            for acc, pp in ((accN, pn), (accD, pd)):
                                if d == 1:
                                    S.op("dve", lambda e, acc=acc, pp=pp, d=d, g=g: e.tensor_copy(out=acc_ap(acc[:], d, g), in_=ps_ap(pp[:], d)),
                                         R=[pp], W=[acc])
                                else:
                                    S.op("dve", lambda e, acc=acc, pp=pp, d=d, g=g: e.tensor_tensor(
                                        out=acc_ap(acc[:], d, g), in0=acc_ap(acc[:], d, g), in1=ps_ap(pp[:], d), op=ALU.add),
                                         R=[pp, acc], W=[acc])

                    for n in range(len(qlist) + 1):
                        if n < len(qlist):
                            stage_S(n)
                        if n >= 1:
                            stage_PV(n - 1)
                    S.op("act", lambda e: e.activation(out=accD[:], in_=accD[:], func=AF.Ln), R=[accD], W=[accD])
                    S.op("act", lambda e: e.activation(out=accD[:], in_=accD[:], func=AF.Exp, scale=-1.0), R=[accD], W=[accD])
                    S.op("dve", lambda e, hp=hp: e.tensor_tensor(out=attT[:, hp, :], in0=accN[:], in1=accD[:], op=ALU.mult),
                         R=[accN, accD], W=[(attT, hp)])
            with ExitStack() as e3:
                fm_rmsnorm(attT, lambda m: attN[:, m, :], gaos, 4, e3, attN)
                S.dma(attd, attN[:].rearrange("q m f -> q (m f)"), R=[attN], W=[attdb])
                if DEBUG:
                    dbg_dump("att", attN, lambda m: attN[:, m, :], e3)

        with ExitStack() as esC:
            wupb = sb("wupb", [128, 8, 4096], BF16, esC)
            wdnb = sb("wdnb", [128, 32, 1024], BF16, esC)
            x1b = Buf(None)
            with ExitStack() as e1:
                wo = sb("wo", [128, 8, 1024], BF16, e1)
                for m in range(8):
                    load_w(wo, lambda m=m: wo[:, m, :], w_out[m * 128:(m + 1) * 128, :], 1024, key=m)
                ssmN2 = sb("ssmN2", [128, 4, 2048], BF16, e1)
                S.dma(ssmN2[:].rearrange("q m f -> q (m f)"), ssmd, R=[ssmdb], W=[ssmN2])
                attN2 = sb("attN2", [128, 4, 2048], BF16, e1)
                S.dma(attN2[:].rearrange("q m f -> q (m f)"), attd, R=[attdb], W=[attN2])
                for dc in range(8):
                    for q4 in range(4):
                        load_w(wupb, lambda dc=dc, q4=q4: wupb[:, dc, 1024 * q4:1024 * q4 + 1024],
                               w_up[dc * 128:(dc + 1) * 128, 1024 * q4:1024 * q4 + 1024], 1024, g2s, dc, key=(dc, q4),
                               eng="pool")
                for fc in range(32):
                    load_w(wdnb, lambda fc=fc: wdnb[:, fc, :], w_down[fc * 128:(fc + 1) * 128, :], 1024, key=fc,
                           eng="pool")
                xts = [sb("xtC%d" % i, [128, 1024], F32, e1) for i in range(3)]
                for j in range(16):
                    xt = xts[j % 3]
                    S.dma(xt[:], xm_t[:, j, :], W=[xt])
                    for hf_ in range(2):
                        pf = nextF()
                        for m in range(8):
                            S.op("pe", lambda e, m=m, j=j, hf_=hf_, pf=pf: e.matmul(
                                pf[:], lhsT=(attN2[:, m, :].rearrange("q (p j) -> q p j", j=16)[:, :, j] if m < 4 else ssmN2[:, m - 4, 128 * j:128 * j + 128]),
                                rhs=wo[:, m, 512 * hf_:512 * hf_ + 512],
                                start=(m == 0), stop=(m == 7)), R=[attN2, ssmN2, (wo, m)], W=[pf])
                        S.op("dve", lambda e, hf_=hf_, pf=pf, xt=xt: e.tensor_tensor(
                            out=xt[:, 512 * hf_:512 * hf_ + 512], in0=xt[:, 512 * hf_:512 * hf_ + 512], in1=pf[:], op=ALU.add),
                             R=[pf, xt], W=[xt])
                    S.dma(x1_t[:, j, :], xt[:], R=[xt], W=[(x1b, j)])
            xn2T = sb("xn2T", [128, 8, 2048], BF16, esC)
            hT = sb("hT", [128, 32, 256], BF16, esC)
            rl = [sb("rl%d" % i, [128, 256], BF16, esC) for i in range(2)]
            xts = [sb("xtD%d" % i, [128, 1024], F32, esC) for i in range(3)]
            xnb = [sb("xnbD%d" % i, [128, 1024], BF16, esC) for i in range(2)]
            sq_of = {}
            for n in range(17):
                if n < 16:
                    S.dma(xts[n % 3][:], x1_t[:, n, :], R=[(x1b, n)], W=[xts[n % 3]])
                    sq_of[n] = rms_stats(xts[n % 3], xnb[n % 2])
                if n >= 1:
                    j = n - 1
                    rms_apply(xts[j % 3], xnb[j % 2], sq_of[j])
                    transpose8(xnb[j % 2], lambda j=j: xn2T[:, :, 128 * j:128 * j + 128], xn2T, j // 2, g2s)
            otoks = []
            for n in range(8):
                for fc in range(32):
                    pf = nextF()
                    for dc in range(8):
                        S.op("pe", lambda e, n=n, fc=fc, dc=dc, pf=pf: e.matmul(
                            pf[:, 0:256], lhsT=wupb[:, dc, 128 * fc:128 * fc + 128], rhs=xn2T[:, dc, 256 * n:256 * n + 256],
                            start=(dc == 0), stop=(dc == 7)), R=[(wupb, (dc, fc // 8)), (xn2T, n)], W=[pf])
                    r_ = rl[fc % 2]
                    S.op("act", lambda e, pf=pf, r_=r_: e.activation(out=r_[:], in_=pf[:, 0:256], func=AF.Relu), R=[pf], W=[r_])
                    S.op("pool", lambda e, fc=fc, r_=r_: e.tensor_tensor(out=hT[:, fc, :], in0=r_[:], in1=r_[:], op=ALU.mult),
                         R=[r_], W=[(hT, fc)])
                for jt in range(2):
                    j = 2 * n + jt
                    xt = xts[j % 2]
                    S.dma(xt[:], x1_t[:, j, :], R=[(x1b, j)], W=[xt])
                    for hf_ in range(2):
                        pf = nextF()
                        for fc in range(32):
                            S.op("pe", lambda e, fc=fc, jt=jt, hf_=hf_, pf=pf: e.matmul(
                                pf[:], lhsT=hT[:, fc, 128 * jt:128 * jt + 128], rhs=wdnb[:, fc, 512 * hf_:512 * hf_ + 512],
                                start=(fc == 0), stop=(fc == 31)), R=[(hT, fc), (wdnb, fc)], W=[pf])
                        S.op("dve", lambda e, hf_=hf_, pf=pf, xt=xt: e.tensor_tensor(
                            out=xt[:, 512 * hf_:512 * hf_ + 512], in0=xt[:, 512 * hf_:512 * hf_ + 512], in1=pf[:], op=ALU.add),
                             R=[pf, xt], W=[xt])
                    otoks.append(S.dma(out_t[:, j, :], xt[:], R=[xt]))
            S.final_wait("sp", otoks)
        S.run()
    return nc


def _stack(a, b):
    return np.ascontiguousarray(np.concatenate([a, b], axis=0).astype(np.float32))


def kernel(**inputs):
    f = lambda k: np.asarray(inputs[k], dtype=np.float32)
    x = f("x")[0]
    col = lambda v, n: np.ascontiguousarray(v.reshape(n, 128).T)
    lr = f("ssm_a_re")[0].T
    li = f("ssm_a_im")[0].T
    ldt = np.broadcast_to(f("ssm_log_dt")[0][None, :], (128, 32))
    bre = f("ssm_b_re")[0].transpose(1, 0, 2)
    bim = f("ssm_b_im")[0].transpose(1, 0, 2)
    cre = f("ssm_c_re")[0].transpose(2, 0, 1)
    cim = f("ssm_c_im")[0].transpose(2, 0, 1)
    sgn = np.zeros((128, 2), np.float32)
    sgn[:64, 0] = -1; sgn[64:, 0] = 1; sgn[:64, 1] = 1; sgn[64:, 1] = -1
    kk = np.arange(128)
    tmask = ((kk[None, :] // 16) >= (kk[:, None] // 16)).astype(np.float32)
    mprev = np.where(kk[:, None] >= kk[None, :], 0.0, NEG).astype(np.float32)
    mcur = np.where(kk[:, None] <= kk[None, :], 0.0, NEG).astype(np.float32)
    maskA = np.concatenate([mprev, mcur], axis=1)
    selA = np.zeros((128, 128), np.float32); selA[:, :64] = 1
    selB = np.zeros((128, 128), np.float32); selB[:, 64:] = 1
    common = {
        "w_in": f("w_in")[0], "w_out": f("w_out")[0], "w_up": f("w_mlp_up")[0], "w_down": f("w_mlp_down")[0],
        "glu_w": f("glu_w")[0],
        "g1": col(f("norm1_g")[0], 8), "g2": col(f("norm2_g")[0], 8),
        "gao": col(f("attn_out_norm_g")[0], 4), "gso": col(f("ssm_out_norm_g")[0], 4),
        "glub": col(f("glu_b")[0], 4),
        "gq": np.ascontiguousarray(np.tile(f("q_norm_g")[0], 2)[:, None]),
        "gk": np.ascontiguousarray(np.tile(f("k_norm_g")[0], 2)[:, None]),
        "p_lr": _stack(lr, lr), "p_li": _stack(li, li), "p_ldt": np.ascontiguousarray(ldt),
        "p_bs": _stack(bre, bim), "p_bw": _stack(bim, bre),
        "p_cs": _stack(cre, cim), "p_cw": _stack(cim, cre),
        "p_d": np.ascontiguousarray(np.broadcast_to(f("ssm_d")[0].reshape(1, 512), (128, 512))),
        "c_sgn": sgn, "c_ident": np.eye(128, dtype=np.float32), "c_tmask": tmask,
        "c_maskA": maskA, "c_selA": selA, "c_selB": selB,
    }
    in_maps = []
    for c in range(NCORES):
        m = dict(common)
        m["xm"] = np.ascontiguousarray(x[c * NT:(c + 1) * NT])
        m["xh"] = np.ascontiguousarray(x[(c - 1) * NT:c * NT]) if c > 0 else np.zeros((NT, 1024), np.float32)
        mh = maskA.copy()
        if c == 0:
            mh[:, :128] = NEG
        m["c_maskH"] = mh
        xpp = np.zeros((7, NT, 1024), np.float32)
        for i in range(7):
            cc = c - 7 + i
            if cc >= 0:
                xpp[i] = x[cc * NT:(cc + 1) * NT]
        m["xp"] = xpp
        in_maps.append(m)
    nc = build_nc()
    res = run_bass_kernel_spmd(nc, in_maps, core_ids=list(range(NCORES)))
    kernel.last = res
    out = np.concatenate([res.results[c]["out"] for c in range(NCORES)], axis=0)
    return out.reshape(1, NCORES * NT, 1024).astype(np.float32)
```
